# Optimizing a Trainium2 kernel written in Bass

```python
import math
import jax, jax.numpy as jnp
from jax import lax
import numpy as np

D_MODEL = 1024
BATCH = 8
SEQ = 4096
DEPTH = 2

N_A_LAYERS = DEPTH // 2
N_B_LAYERS = DEPTH - N_A_LAYERS
SSM_EXPAND = 2
D_INNER = SSM_EXPAND * D_MODEL
SSM_HEAD_DIM = 64
SSM_HEADS = D_INNER // SSM_HEAD_DIM
SSM_GROUPS = 4
SSM_STATE = 128
SSM_CONV = 4
SSM_CHUNK = 128
GN = SSM_GROUPS * SSM_STATE
CONV_DIM = D_INNER + 2 * GN
IN_PROJ_DIM = D_INNER + CONV_DIM + SSM_HEADS
SB_HEADS = 16
SB_HEAD_DIM = D_MODEL // SB_HEADS
SB_BLOCK = 128
D_FF = 2816
FFN_CONV = 3
EPS = 1e-6

kernel_name = 'yoco_mamba2_stickbreaking_convffn'


def rmsnorm(x, w):
    xf = x.astype(jnp.float32)
    y = xf * lax.rsqrt(jnp.mean(xf * xf, axis=-1, keepdims=True) + EPS)
    return (y * w.astype(jnp.float32)).astype(x.dtype)


def causal_dwconv(x, w, b):
    k_w, s = w.shape[0], x.shape[1]
    xp = jnp.pad(x, ((0, 0), (k_w - 1, 0), (0, 0)))
    out = b
    for j in range(k_w):
        out = out + xp[:, j:j + s, :] * w[j]
    return out


def segsum(a):
    t = a.shape[-1]
    cs = jnp.cumsum(a, axis=-1)
    diff = cs[..., :, None] - cs[..., None, :]
    mask = jnp.tril(jnp.ones((t, t), dtype=bool))
    return jnp.where(mask, diff, -jnp.inf)


def ssd_chunked(xs, dt, a, bm, cm):
    bsz, s, h, p = xs.shape
    g, n = bm.shape[2], bm.shape[3]
    r = h // g
    c, l = s // SSM_CHUNK, SSM_CHUNK
    xd = (xs * dt[..., None]).reshape(bsz, c, l, g, r, p)
    ad = (dt * a).reshape(bsz, c, l, g, r).transpose(0, 3, 4, 1, 2)
    bc = bm.reshape(bsz, c, l, g, n)
    cc = cm.reshape(bsz, c, l, g, n)
    a_cs = jnp.cumsum(ad, axis=-1)
    lmat = jnp.exp(segsum(ad))
    cb = jnp.einsum('bclgn,bcsgn->bcgls', cc, bc)
    y_diag = jnp.einsum('bcgls,bgrcls,bcsgrp->bclgrp', cb, lmat, xd)
    decay_states = jnp.exp(a_cs[..., -1:] - a_cs)
    states = jnp.einsum('bclgn,bgrcl,bclgrp->bcgrpn', bc, decay_states, xd)
    chunk_decay = jnp.exp(a_cs[..., -1])

    def step(carry, inp):
        st, dc = inp
        return carry * dc[..., None, None] + st, carry

    init = jnp.zeros((bsz, g, r, p, n), dtype=states.dtype)
    _, prev = lax.scan(step, init, (jnp.moveaxis(states, 1, 0), jnp.moveaxis(chunk_decay, -1, 0)))
    prev = jnp.moveaxis(prev, 0, 1)
    y_off = jnp.einsum('bclgn,bcgrpn,bgrcl->bclgrp', cc, prev, jnp.exp(a_cs))
    return (y_diag + y_off).reshape(bsz, s, h, p)


def mamba2_mixer(u, w_in, conv_w, conv_b, dt_bias, a_log, d_skip, gate_norm_w, w_out):
    bsz, s, _ = u.shape
    f32 = jnp.float32
    zxbcdt = u @ w_in
    z = zxbcdt[..., :D_INNER]
    xbc = zxbcdt[..., D_INNER:D_INNER + CONV_DIM]
    dt = zxbcdt[..., D_INNER + CONV_DIM:]
    xbc = jax.nn.silu(causal_dwconv(xbc, conv_w, conv_b))
    xs = xbc[..., :D_INNER].reshape(bsz, s, SSM_HEADS, SSM_HEAD_DIM).astype(f32)
    bm = xbc[..., D_INNER:D_INNER + GN].reshape(bsz, s, SSM_GROUPS, SSM_STATE).astype(f32)
    cm = xbc[..., D_INNER + GN:].reshape(bsz, s, SSM_GROUPS, SSM_STATE).astype(f32)
    dt = jax.nn.softplus(dt.astype(f32) + dt_bias.astype(f32))
    a = -jnp.exp(a_log.astype(f32))
    y = ssd_chunked(xs, dt, a, bm, cm)
    y = y + d_skip.astype(f32)[:, None] * xs
    y = y.reshape(bsz, s, D_INNER) * jax.nn.silu(z.astype(f32))
    gsz = D_INNER // SSM_GROUPS
    y = rmsnorm(y.reshape(bsz, s, SSM_GROUPS, gsz), gate_norm_w.reshape(SSM_GROUPS, gsz))
    return y.reshape(bsz, s, D_INNER).astype(u.dtype) @ w_out


def stick_breaking_attention(q, k, v):
    bsz, s, h, d = q.shape
    nblk = s // SB_BLOCK
    scale = 1.0 / math.sqrt(d)
    qb = q.reshape(bsz, nblk, SB_BLOCK, h, d).transpose(1, 0, 3, 2, 4)
    kt = k.transpose(0, 2, 1, 3).astype(jnp.float32)
    vt = v.transpose(0, 2, 1, 3).astype(jnp.float32)
    starts = jnp.arange(nblk, dtype=jnp.int32) * SB_BLOCK
    kpos = jnp.arange(s, dtype=jnp.int32)

    def block(args):
        qblk, i0 = args
        qpos = i0 + jnp.arange(SB_BLOCK, dtype=jnp.int32)
        mask = kpos[None, :] < qpos[:, None]
        zl = jnp.einsum('bhqd,bhkd->bhqk', qblk.astype(jnp.float32), kt) * scale
        log_1m = jnp.where(mask, -jax.nn.softplus(zl), 0.0)
        later = lax.cumsum(log_1m, axis=3, reverse=True) - log_1m
        att = jnp.where(mask, jnp.exp(jax.nn.log_sigmoid(zl) + later), 0.0)
        return jnp.einsum('bhqk,bhkd->bhqd', att, vt)

    o = lax.map(block, (qb, starts))
    return o.transpose(1, 0, 3, 2, 4).reshape(bsz, s, h * d).astype(q.dtype)


def conv_ffn(u, w_up, conv_w, conv_b, w_down):
    hid = causal_dwconv(u @ w_up, conv_w, conv_b)
    gate, val = hid[..., :D_FF], hid[..., D_FF:]
    return (jax.nn.silu(gate) * val) @ w_down


def setup_inputs(seed: int = 0) -> dict:
    key = jax.random.key(seed)
    ks = jax.random.split(key, 24)
    f32 = jnp.float32

    def nrm(k, shape, scale):
        return jax.random.normal(k, shape, f32) * scale

    def gain(k, shape):
        return 1.0 + 0.02 * jax.random.normal(k, shape, f32)

    na, nb = N_A_LAYERS, N_B_LAYERS
    dt0 = jnp.exp(jax.random.uniform(ks[5], (na, SSM_HEADS), f32, math.log(1e-3), math.log(1e-1)))
    dt_bias = dt0 + jnp.log(-jnp.expm1(-dt0))
    a_log = jnp.log(jax.random.uniform(ks[6], (na, SSM_HEADS), f32, 1.0, 16.0))
    return {
        'x': jax.random.normal(ks[0], (BATCH, SEQ, D_MODEL), f32),
        'ssm_norm_w': gain(ks[1], (na, D_MODEL)),
        'ssm_in_w': nrm(ks[2], (na, D_MODEL, IN_PROJ_DIM), D_MODEL ** -0.5),
        'ssm_conv_w': nrm(ks[3], (na, SSM_CONV, CONV_DIM), SSM_CONV ** -0.5),
        'ssm_conv_b': nrm(ks[4], (na, CONV_DIM), 0.02),
        'ssm_dt_bias': dt_bias,
        'ssm_a_log': a_log,
        'ssm_d': 1.0 + 0.1 * jax.random.normal(ks[7], (na, SSM_HEADS), f32),
        'ssm_gate_norm_w': gain(ks[8], (na, D_INNER)),
        'ssm_out_w': nrm(ks[9], (na, D_INNER, D_MODEL), D_INNER ** -0.5),
        'kv_norm_w': gain(ks[10], (D_MODEL,)),
        'w_k': nrm(ks[11], (D_MODEL, SB_HEADS * SB_HEAD_DIM), D_MODEL ** -0.5),
        'w_v': nrm(ks[12], (D_MODEL, SB_HEADS * SB_HEAD_DIM), D_MODEL ** -0.5),
        'attn_norm_w': gain(ks[13], (nb, D_MODEL)),
        'w_q': nrm(ks[14], (nb, D_MODEL, SB_HEADS * SB_HEAD_DIM), D_MODEL ** -0.5),
        'w_o': nrm(ks[15], (nb, SB_HEADS * SB_HEAD_DIM, D_MODEL), D_MODEL ** -0.5),
        'ffn_norm_w': gain(ks[16], (DEPTH, D_MODEL)),
        'ffn_up_w': nrm(ks[17], (DEPTH, D_MODEL, 2 * D_FF), D_MODEL ** -0.5),
        'ffn_conv_w': nrm(ks[18], (DEPTH, FFN_CONV, 2 * D_FF), FFN_CONV ** -0.5),
        'ffn_conv_b': nrm(ks[19], (DEPTH, 2 * D_FF), 0.02),
        'ffn_down_w': nrm(ks[20], (DEPTH, D_FF, D_MODEL), D_FF ** -0.5),
        'final_norm_w': gain(ks[21], (D_MODEL,)),
    }


def reference(x, ssm_norm_w, ssm_in_w, ssm_conv_w, ssm_conv_b, ssm_dt_bias, ssm_a_log, ssm_d,
              ssm_gate_norm_w, ssm_out_w, kv_norm_w, w_k, w_v, attn_norm_w, w_q, w_o,
              ffn_norm_w, ffn_up_w, ffn_conv_w, ffn_conv_b, ffn_down_w, final_norm_w):
    bsz, s, _ = x.shape
    h = x
    k_shared = None
    v_shared = None
    for layer in range(DEPTH):
        if layer < N_A_LAYERS:
            i = layer
            h = h + mamba2_mixer(rmsnorm(h, ssm_norm_w[i]), ssm_in_w[i], ssm_conv_w[i], ssm_conv_b[i],
                                 ssm_dt_bias[i], ssm_a_log[i], ssm_d[i], ssm_gate_norm_w[i], ssm_out_w[i])
        else:
            i = layer - N_A_LAYERS
            if k_shared is None:
                hk = rmsnorm(h, kv_norm_w)
                k_shared = (hk @ w_k).reshape(bsz, s, SB_HEADS, SB_HEAD_DIM)
                v_shared = (hk @ w_v).reshape(bsz, s, SB_HEADS, SB_HEAD_DIM)
            q = (rmsnorm(h, attn_norm_w[i]) @ w_q[i]).reshape(bsz, s, SB_HEADS, SB_HEAD_DIM)
            h = h + stick_breaking_attention(q, k_shared, v_shared) @ w_o[i]
        h = h + conv_ffn(rmsnorm(h, ffn_norm_w[layer]), ffn_up_w[layer], ffn_conv_w[layer],
                         ffn_conv_b[layer], ffn_down_w[layer])
    return rmsnorm(h, final_norm_w)
```

```python
import math
from contextlib import ExitStack

import numpy as np
import concourse.bass as bass
import concourse.mybir as mybir
from concourse.bass_utils import run_bass_kernel_spmd
from concourse.alu_op_type import AluOpType as ALU

F32 = mybir.dt.float32
BF16 = mybir.dt.bfloat16
AF = mybir.ActivationFunctionType

D = 1024
DI = 2048
NH = 32
HP = 64
NG = 4
NS = 128
CONVD = 3072
INP = 5152
DFF = 2816
AH = 16
AD = 64
EPS = 1e-6
TT = 512
NCORES = 8
NCONST = 22
NCB = 6

ENGS = ("pe", "act", "dve", "pool", "sp")
DEBUG_LEVEL = 99
EXTRA_NOPS = 0


class Buf:
    def __init__(self, name, excl=False):
        self.name = name
        self.excl = excl
        self.w = {}
        self.r = {}
        self.regions = {}

    def __getitem__(self, key):
        return (self, key)


def _merge(dst, src):
    for k, (s, v) in src.items():
        if k not in dst or dst[k][1] < v:
            dst[k] = (s, v)


class Prog:
    SEM_LIMIT = 30000

    def __init__(self, nc, stack):
        self.nc = nc
        self.stack = stack
        self.q = {e: [] for e in ENGS}
        self.cur = {}
        self.cnt = {}
        self.waited = {e: {} for e in ENGS}
        self.pending = {e: [] for e in ENGS}
        for e in ("pe", "act", "dve", "pool"):
            self._new_sem(e)
        self.ring = {}
        self.ring_i = {}
        self.R = 8
        for e in ("sp", "pool", "act"):
            self.ring[e] = [stack.enter_context(nc.semaphore(f"dq_{e}_{i}")) for i in range(self.R)]
            self.ring_i[e] = 0
        self.nsem = 0

    def _new_sem(self, e):
        n = getattr(self, "_semn", 0)
        self._semn = n + 1
        self.cur[e] = self.stack.enter_context(self.nc.semaphore(f"s_{e}_{n}"))
        self.cnt[e] = 0

    @staticmethod
    def _split(b):
        if isinstance(b, tuple):
            return b[0], b[1]
        return b, None

    def _deps_read(self, b, deps):
        buf, key = self._split(b)
        _merge(deps, buf.w)
        if key is None:
            for w, _ in buf.regions.values():
                _merge(deps, w)
        elif key in buf.regions:
            _merge(deps, buf.regions[key][0])

    def _deps_write(self, b, deps):
        buf, key = self._split(b)
        _merge(deps, buf.w)
        _merge(deps, buf.r)
        if key is None:
            for w, r in buf.regions.values():
                _merge(deps, w)
                _merge(deps, r)
        elif key in buf.regions:
            _merge(deps, buf.regions[key][0])
            _merge(deps, buf.regions[key][1])

    def _note_read(self, b, tok):
        buf, key = self._split(b)
        if key is None:
            _merge(buf.r, tok)
        else:
            reg = buf.regions.setdefault(key, [{}, {}])
            _merge(reg[1], tok)

    def _note_write(self, b, tok):
        buf, key = self._split(b)
        if key is None:
            buf.w = dict(tok)
            buf.r = {}
            buf.regions = {}
        else:
            buf.regions[key] = [dict(tok), {}]

    @staticmethod
    def rows(buf, tt, n=8):
        return [buf[(tt, m)] for m in range(n)]

    def op(self, eng, fn, reads=(), writes=(), inc=True, dma=False):
        rdeps, wdeps = {}, {}
        xreads = [b for b in reads if self._split(b)[0].excl]
        reads = [b for b in reads if not self._split(b)[0].excl]
        for b in reads:
            self._deps_read(b, rdeps)
        for b in xreads:
            self._deps_read(b, rdeps)
            self._deps_write(b, wdeps)
        for b in writes:
            self._deps_write(b, wdeps)
        own = id(self.cur[eng]) if eng in self.cur else None
        deps = {}
        for k, (s, v) in rdeps.items():
            if k == own and eng == "pe":
                continue
            if k not in deps or deps[k][1] < v:
                deps[k] = (s, v)
        for k, (s, v) in wdeps.items():
            if k == own:
                continue
            if k not in deps or deps[k][1] < v:
                deps[k] = (s, v)
        wd = self.waited[eng]
        for k, (s, v) in deps.items():
            if wd.get(k, 0) < v:
                wd[k] = v
                self.q[eng].append(lambda e, s=s, v=v: e.wait_ge(s, v))
        if dma:
            i = self.ring_i[eng]
            self.ring_i[eng] = i + 1
            sem = self.ring[eng][i % self.R]
            val = 16 * (i // self.R + 1)
            if i >= self.R:
                prev = 16 * (i // self.R)
                if wd.get(id(sem), 0) < prev:
                    wd[id(sem)] = prev
                    self.q[eng].append(lambda e, s=sem, v=prev: e.wait_ge(s, v))
            self.q[eng].append(lambda e, s=sem: fn(e).then_inc(s, 16))
            tok = {id(sem): (sem, val)}
        else:
            if self.cnt[eng] >= self.SEM_LIMIT and not self.pending[eng]:
                self._new_sem(eng)
            sem = self.cur[eng]
            val = self.cnt[eng] + 1
            tok = {id(sem): (sem, val)}
            if inc:
                self.cnt[eng] = val
                self.q[eng].append(lambda e, s=sem: fn(e).then_inc(s, 1))
                self.pending[eng] = []
            else:
                self.q[eng].append(lambda e: fn(e))
                self.pending[eng].append(1)
        for b in reads:
            self._note_read(b, tok)
        for b in xreads:
            self._note_read(b, tok)
        for b in writes:
            self._note_write(b, tok)
        return tok

    def final_wait(self, eng, toks):
        for tok in toks:
            for k, (s, v) in tok.items():
                self.q[eng].append(lambda e, s=s, v=v: e.wait_ge(s, v))

    def barrier(self):
        assert all(not p for p in self.pending.values())
        toks = []
        for e in ("pe", "act", "dve", "pool"):
            if self.cnt[e] > 0:
                toks.append((self.cur[e], self.cnt[e]))
        for eng in self.ring:
            n = self.ring_i[eng]
            for r in range(min(n, self.R)):
                toks.append((self.ring[eng][r], 16 * ((n - 1 - r) // self.R + 1)))
        for eng in ENGS:
            wd = self.waited[eng]
            for s, v in toks:
                if wd.get(id(s), 0) < v:
                    wd[id(s)] = v
                    self.q[eng].append(lambda e, s=s, v=v: e.wait_ge(s, v))

    def drain_dma(self):
        for eng in self.ring:
            n = self.ring_i[eng]
            for r in range(min(n, self.R)):
                cnt = (n - 1 - r) // self.R + 1
                self.q[eng].append(lambda e, s=self.ring[eng][r], v=16 * cnt: e.wait_ge(s, v))

    def emit(self):
        assert all(not p for p in self.pending.values()), "dangling inc=False op"
        with self.nc.Block() as block:
            @block.tensor
            def _(e):
                for f in self.q["pe"]:
                    f(e)

            @block.scalar
            def _(e):
                for f in self.q["act"]:
                    f(e)

            @block.vector
            def _(e):
                for f in self.q["dve"]:
                    f(e)

            @block.gpsimd
            def _(e):
                for f in self.q["pool"]:
                    f(e)

            @block.sync
            def _(e):
                for f in self.q["sp"]:
                    f(e)


def run_pipeline(units, nstage, oldest_first=False, order=None):
    n = len(units)
    for i in range(n + nstage - 1):
        for s_ in (order if order is not None else (range(nstage - 1, -1, -1) if oldest_first else range(nstage))):
            u = i - s_
            if 0 <= u < n and units[u][s_] is not None:
                units[u][s_]()


def _pk(v, nchunk):
    return np.ascontiguousarray(np.asarray(v, np.float32).reshape(nchunk, 128).T)


class VecLayout:
    def __init__(self):
        self.cols = {}
        self.n = 0

    def add(self, name, width):
        self.cols[name] = (self.n, width)
        self.n += width

    def sl(self, name):
        a, w = self.cols[name]
        return slice(a, a + w)


VL = VecLayout()
for _n, _w in [("ssm_norm", 8), ("kv_norm", 8), ("attn_norm", 8), ("ffn_norm0", 8), ("ffn_norm1", 8),
               ("final_norm", 8), ("ssm_conv_w", 24 * 4), ("ssm_conv_b", 24), ("ssm_d", 16),
               ("gate_norm", 16), ("ffn_conv_w0", 44 * 3), ("ffn_conv_b0", 44), ("ffn_conv_w1", 44 * 3),
               ("ffn_conv_b1", 44), ("dt_bias", 32), ("a_log", 32), ("eps", 1), ("one", 1), ("ssm_d_bc", 32)]:
    VL.add(_n, _w)


def pack_vecs(inp):
    out = np.zeros((128, VL.n), np.float32)
    out[:, VL.sl("ssm_norm")] = _pk(inp["ssm_norm_w"][0], 8)
    out[:, VL.sl("kv_norm")] = _pk(inp["kv_norm_w"], 8)
    out[:, VL.sl("attn_norm")] = _pk(inp["attn_norm_w"][0], 8)
    out[:, VL.sl("ffn_norm0")] = _pk(inp["ffn_norm_w"][0], 8)
    out[:, VL.sl("ffn_norm1")] = _pk(inp["ffn_norm_w"][1], 8)
    out[:, VL.sl("final_norm")] = _pk(inp["final_norm_w"], 8)
    cw = np.asarray(inp["ssm_conv_w"][0], np.float32)
    out[:, VL.sl("ssm_conv_w")] = cw.reshape(4, 24, 128).transpose(2, 1, 0).reshape(128, 96)
    out[:, VL.sl("ssm_conv_b")] = _pk(inp["ssm_conv_b"][0], 24)
    out[:, VL.sl("ssm_d")] = _pk(np.repeat(np.asarray(inp["ssm_d"][0], np.float32), HP), 16)
    out[:, VL.sl("gate_norm")] = _pk(inp["ssm_gate_norm_w"][0], 16)
    for l in range(2):
        fw = np.asarray(inp["ffn_conv_w"][l], np.float32)
        out[:, VL.sl(f"ffn_conv_w{l}")] = fw.reshape(3, 44, 128).transpose(2, 1, 0).reshape(128, 132)
        out[:, VL.sl(f"ffn_conv_b{l}")] = _pk(inp["ffn_conv_b"][l], 44)
    out[:, VL.sl("dt_bias")] = np.broadcast_to(np.asarray(inp["ssm_dt_bias"][0], np.float32), (128, 32))
    out[:, VL.sl("ssm_d_bc")] = np.broadcast_to(np.asarray(inp["ssm_d"][0], np.float32), (128, 32))
    out[:, VL.sl("eps")] = EPS
    out[:, VL.sl("one")] = 1.0
    out[:, VL.sl("a_log")] = np.broadcast_to(np.asarray(inp["ssm_a_log"][0], np.float32), (128, 32))
    return out


def make_consts():
    k = np.arange(128)
    c = np.zeros((128, NCONST, 128), np.float32)
    c[:, 0, :] = np.eye(128)
    c[:, 1, :] = (k[:, None] <= k[None, :])
    c[:, 2, :] = (k[:, None] > k[None, :])
    c[:, 3, :] = 1.0
    c[:, 4, :] = (k[:, None] >= k[None, :])
    c[:, 5, :] = (k[:, None] < k[None, :])
    q = np.arange(TT)
    for j in range(4):
        c[:, NCB + 4 * j:NCB + 4 + 4 * j, :] = ((k[:, None] + 128 * j) < q[None, :]).astype(np.float32).reshape(128, 4, 128)
    return c.reshape(128, NCONST * 128)


class Builder:
    def __init__(self, T, phases, debug=()):
        self.T = T
        self.NT = T // TT
        self.phases = phases
        self.debug = debug

    def dram(self, name, shape, dt, kind="Internal"):
        if kind == "Internal" and name in getattr(self, "extra_out", ()):
            kind = "ExternalOutput"
        return self.nc.dram_tensor(name, list(shape), dt, kind=kind).ap()

    def build(self):
        nc = bass.Bass("TRN2", target_bir_lowering=False)
        self.nc = nc
        T = self.T
        self.xT = self.dram("xT", [D, T], F32, "ExternalInput")
        self.vecs_d = self.dram("vecs", [128, VL.n], F32, "ExternalInput")
        self.consts_d = self.dram("consts", [128, NCONST * 128], F32, "ExternalInput")
        self.w = {}
        for name, shape in [("ssm_in_w", [D, INP]), ("ssm_out_w", [DI, D]), ("w_k", [D, D]), ("w_v", [D, D]),
                            ("w_q", [D, D]), ("w_o", [D, D]), ("ffn_up_w0", [D, 2 * DFF]),
                            ("ffn_up_w1", [D, 2 * DFF]), ("ffn_down_w0", [DFF, D]), ("ffn_down_w1", [DFF, D])]:
            self.w[name] = self.dram(name, shape, F32, "ExternalInput")
        self.outT = self.dram("outT", [D, T], F32, "ExternalOutput")
        self.hA = self.dram("hA", [D, T], F32)
        self.hB = self.dram("hB", [D, T], F32)
        with ExitStack() as stack:
            self.stack = stack
            P = Prog(nc, stack)
            self.P = P
            self.vecs = stack.enter_context(nc.sbuf_tensor("vecs_sb", [128, VL.n], F32))
            self.cf = stack.enter_context(nc.sbuf_tensor("cf_sb", [128, NCB, 128], F32))
            self.cb = stack.enter_context(nc.sbuf_tensor("cb_sb", [128, NCB, 128], BF16))
            self.b_const = Buf("const")
            P.op("sp", lambda e: e.dma_start(out=self.vecs[:], in_=self.vecs_d), writes=[self.b_const], dma=True)
            P.op("sp", lambda e: e.dma_start(out=self.cf[:].rearrange("p a b -> p (a b)"), in_=self.consts_d[:, 0:NCB * 128]),
                 writes=[self.b_const], dma=True)
            P.op("pool", lambda e: e.dma_start(out=self.cb[:].rearrange("p a b -> p (a b)"), in_=self.consts_d[:, 0:NCB * 128]),
                 writes=[self.b_const], dma=True)
            for _ in range(EXTRA_NOPS):
                P.op("pool", lambda e: e.memset(self.cb[:, 3, 0:1], 1.0), writes=[self.b_const])
            self.out_toks = []
            for ph in self.phases:
                P.barrier()
                getattr(self, "phase_" + ph[0])(*ph[1:])
            P.drain_dma()
            P.final_wait("sp", self.out_toks)
            P.emit()
        return nc

    def vcol(self, name, i=0, n=1):
        a, _ = VL.cols[name]
        return self.vecs[:, a + i:a + i + n]

    def load_weight_bf16(self, wt, wb, dram_ap, kchunks, ncols, order=None):
        src = dram_ap.rearrange("(k p) n -> p k n", p=128)
        CW = 1024
        blocks = list(range(0, ncols, CW))
        if order is not None:
            blocks = [b for b in order] + [b for b in blocks if b not in order]
        for c0 in blocks:
            c1 = min(ncols, c0 + CW)
            for k in range(kchunks):
                self.P.op("pool", lambda e, k=k, c0=c0, c1=c1: e.dma_start(out=wt[:, k, c0:c1], in_=src[:, k, c0:c1]),
                          writes=[wb[(k, c0)]], dma=True)

    def rms_stats(self, h_sb, b_h, sq, b_sq, ps_stat, b_stat, rstd, b_rstd, nchunk, denom):
        P = self.P
        P.op("act", lambda e: e.activation(out=sq, in_=h_sb, func=AF.Square, scale=1.0 / math.sqrt(denom)),
             reads=[b_h], writes=b_sq)
        for k in range(nchunk):
            P.op("pe", lambda e, k=k: e.matmul(ps_stat[:], lhsT=self.cb[:, 3, :], rhs=sq[:, k, :],
                                               start=(k == 0), stop=(k == nchunk - 1)),
                 reads=b_sq + [self.b_const], writes=[b_stat], inc=(k == nchunk - 1))
        P.op("act", lambda e: e.activation(out=rstd[:], in_=ps_stat[:], func=AF.Ln, bias=self.vcol("eps")),
             reads=[b_stat, self.b_const], writes=[b_rstd])
        P.op("act", lambda e: e.activation(out=rstd[:], in_=rstd[:], func=AF.Exp, scale=-0.5),
             reads=[b_rstd], writes=[b_rstd])

    def dbuf(self, name):
        if not hasattr(self, "_dbufs"):
            self._dbufs = {}
        return self._dbufs.setdefault(name, Buf("dram_" + name))

    def tile_ap(self, name, tt):
        return getattr(self, name)[:, tt * TT:(tt + 1) * TT].rearrange("(k p) t -> p k t", p=128)

    def phase_ffn(self, layer, h_in, h_out):
        nc, P, NT = self.nc, self.P, self.NT
        bd_in, bd_out = self.dbuf(h_in), self.dbuf(h_out)
        NP = DFF // 128
        XW = TT + 2
        with ExitStack() as st:
            sb = lambda n, s, d: st.enter_context(nc.sbuf_tensor(f"f{layer}_{n}", s, d))
            ps = lambda n: st.enter_context(nc.psum_tensor(f"f{layer}_{n}", [128, TT], F32))
            wup = sb("wup", [128, 8, 2 * DFF], BF16)
            wdn = sb("wdn", [128, NP, D], BF16)
            b_wup, b_wdn = Buf("wup"), Buf("wdn")
            order = []
            for j in range(NP):
                for col in (j * 128, DFF + j * 128):
                    if col // 1024 * 1024 not in order:
                        order.append(col // 1024 * 1024)
            self.load_weight_bf16(wup, b_wup, self.w[f"ffn_up_w{layer}"], 8, 2 * DFF, order)
            self.load_weight_bf16(wdn, b_wdn, self.w[f"ffn_down_w{layer}"], NP, D)
            h_sb = sb("h", [128, 8, TT], F32); b_h = Buf("h")
            rstd = sb("rstd", [128, TT], F32); b_rstd = Buf("rstd")
            u = sb("u", [128, 8, TT], BF16); b_u = Buf("u")
            g = sb("g", [128, NP, TT], BF16); b_g = Buf("g")
            scr = sb("scr", [128, 3 * XW + 3 * TT], F32)
            xpad = [scr[:, i * XW:(i + 1) * XW] for i in range(3)]
            acc = [scr[:, 3 * XW + i * TT:3 * XW + (i + 1) * TT] for i in range(3)]
            b_xpad = [Buf(f"xpad{i}") for i in range(3)]
            b_acc = [Buf(f"acc{i}") for i in range(3)]
            sq = scr[:, 0:4 * TT].bitcast(BF16).rearrange("p (k t) -> p k t", k=8)
            b_sq = b_xpad + b_acc
            hres = [sb(f"hres{i}", [128, TT], F32) for i in range(2)]; b_hres = [Buf(f"hres{i}") for i in range(2)]
            hout = [sb(f"hout{i}", [128, TT], F32) for i in range(2)]; b_hout = [Buf(f"hout{i}") for i in range(2)]
            carry = sb("carry", [128, 2 * NP, 2], F32); b_carry = Buf("carry")
            ps_stat = ps("stat"); b_stat = Buf("stat", True)
            ps_x = [ps(f"x{i}") for i in range(4)]; b_psx = [Buf(f"psx{i}", True) for i in range(4)]
            ps_o = [ps(f"o{i}") for i in range(2)]; b_pso = [Buf(f"pso{i}", True) for i in range(2)]
            nw = f"ffn_norm{layer}"
            cwn, cbn = f"ffn_conv_w{layer}", f"ffn_conv_b{layer}"
            P.op("pool", lambda e: e.memset(carry[:], 0.0), writes=[b_carry])

            def load_h(tt):
                P.op("sp", lambda e: e.dma_start(out=h_sb[:], in_=self.tile_ap(h_in, tt)),
                     reads=Prog.rows(bd_in, tt), writes=[b_h], dma=True)

            def prologue(tt):
                self.rms_stats(h_sb[:], b_h, sq, b_sq, ps_stat, b_stat, rstd, b_rstd, 8, float(D))
                for k in range(8):
                    P.op("dve", lambda e, k=k: e.scalar_tensor_tensor(
                        out=u[:, k, :], in0=h_sb[:, k, :], scalar=self.vcol(nw, k), in1=rstd[:],
                        op0=ALU.mult, op1=ALU.mult), reads=[b_h, b_rstd, self.b_const], writes=[b_u[k]])
                if tt + 1 < NT:
                    load_h(tt + 1)

            def down(tt, ms):
                for m in ms:
                    o = m % 2
                    for kk in range(NP):
                        P.op("pe", lambda e, kk=kk, m=m, o=o: e.matmul(
                            ps_o[o][:], lhsT=wdn[:, kk, m * 128:(m + 1) * 128], rhs=g[:, kk, :],
                            start=(kk == 0), stop=(kk == NP - 1)),
                            reads=[b_g[kk], b_wdn[(kk, 0)]], writes=[b_pso[o]], inc=(kk == NP - 1))
                    P.op("sp", lambda e, m=m, o=o: e.dma_start(
                        out=hres[o][:], in_=getattr(self, h_in)[m * 128:(m + 1) * 128, tt * TT:(tt + 1) * TT]),
                        reads=[bd_in[(tt, m)]], writes=[b_hres[o]], dma=True)
                    P.op("dve", lambda e, o=o: e.tensor_tensor(
                        out=hout[o][:], in0=hres[o][:], in1=ps_o[o][:], op=ALU.add),
                        reads=[b_hres[o], b_pso[o]], writes=[b_hout[o]])
                    P.op("pool", lambda e, m=m, o=o: e.dma_start(
                        out=getattr(self, h_out)[m * 128:(m + 1) * 128, tt * TT:(tt + 1) * TT], in_=hout[o][:]),
                        reads=[b_hout[o]], writes=[bd_out[(tt, m)]], dma=True)

            load_h(0)
            prologue(0)
            xi = 0
            for tt in range(NT):
                if DEBUG_LEVEL < 2:
                    break
                for j in range(NP if DEBUG_LEVEL >= 4 else 1):
                    res = []
                    for half in range(2):
                        c = j + half * NP
                        s = xi % 4
                        a = xi % 3
                        xi += 1
                        for k in range(8):
                            P.op("pe", lambda e, k=k, c=c, s=s: e.matmul(
                                ps_x[s][:], lhsT=wup[:, k, c * 128:(c + 1) * 128], rhs=u[:, k, :],
                                start=(k == 0), stop=(k == 7)),
                                reads=[b_u[k], b_wup[(k, (c * 128) // 1024 * 1024)]], writes=[b_psx[s]], inc=(k == 7))
                        P.op("pool", lambda e, c=c, a=a: e.tensor_copy(out=xpad[a][:, 0:2], in_=carry[:, c, :]),
                             reads=[b_carry[c]], writes=[b_xpad[a]["halo"]])
                        P.op("act", lambda e, s=s, a=a: e.copy(out=xpad[a][:, 2:XW], in_=ps_x[s][:]),
                             reads=[b_psx[s]], writes=[b_xpad[a]["body"]])
                        P.op("act", lambda e, c=c, s=s, a=a: e.activation(
                            out=acc[a], in_=ps_x[s][:], func=AF.Identity, scale=self.vcol(cwn, c * 3 + 2),
                            bias=self.vcol(cbn, c)),
                            reads=[b_psx[s], self.b_const], writes=[b_acc[a]])
                        for tap in (1, 0):
                            P.op("dve", lambda e, c=c, a=a, tap=tap: e.scalar_tensor_tensor(
                                out=acc[a], in0=xpad[a][:, tap:tap + TT], scalar=self.vcol(cwn, c * 3 + tap),
                                in1=acc[a], op0=ALU.mult, op1=ALU.add),
                                reads=[b_xpad[a], b_acc[a], self.b_const], writes=[b_acc[a]])
                        if DEBUG_LEVEL >= 3:
                            P.op("pool", lambda e, c=c, a=a: e.tensor_copy(out=carry[:, c, :], in_=xpad[a][:, TT:XW]),
                                 reads=[b_xpad[a]], writes=[b_carry[c]])
                        res.append(a)
                    P.op("act", lambda e, a=res[0]: e.activation(out=acc[a], in_=acc[a], func=AF.Silu),
                         reads=[b_acc[res[0]]], writes=[b_acc[res[0]]])
                    P.op("dve", lambda e, j=j, a0=res[0], a1=res[1]: e.tensor_tensor(
                        out=g[:, j, :], in0=acc[a0], in1=acc[a1], op=ALU.mult),
                        reads=[b_acc[res[0]], b_acc[res[1]]], writes=[b_g[j]])
                if DEBUG_LEVEL < 5:
                    break
                down(tt, range(0, 4))
                if tt + 1 < NT:
                    prologue(tt + 1)
                down(tt, range(4, 8))

    def phase_m_in(self, h_in):
        nc, P, NT, T = self.nc, self.P, self.NT, self.T
        bd_in = self.dbuf(h_in)
        for n, shp, dt in [("zsT", [DI, T], F32), ("xs_tm", [T, DI], BF16),
                           ("BT", [NG * NS, T], BF16), ("CT", [NG * NS, T], BF16),
                           ("B_tm", [T, NG * NS], BF16), ("dt_tm", [T, NH], F32)]:
            setattr(self, n, self.dram(n, shp, dt))
        bd = {n: self.dbuf(n) for n in ("zsT", "xs_tm", "BT", "CT", "B_tm", "dt_tm")}
        XW = TT + 3
        NB = TT // 128
        NXP, NAC, NXC, NZO = 3, 5, 3, 3
        with ExitStack() as st:
            sb = lambda n, s, d: st.enter_context(nc.sbuf_tensor(f"mi_{n}", s, d))
            win = sb("win", [128, 8, INP], BF16); b_win = Buf("win")
            self.load_weight_bf16(win, b_win, self.w["ssm_in_w"], 8, INP)
            h_sb = sb("h", [128, 8, TT], F32); b_h = Buf("h")
            rstd = sb("rstd", [128, TT], F32); b_rstd = Buf("rstd")
            u2 = [sb(f"u{i}", [128, 8, TT], BF16) for i in range(2)]; b_u2 = [Buf(f"u{i}") for i in range(2)]
            sq = sb("sq", [128, 8, TT], BF16); b_sq = [Buf("sq")]
            xpad = [sb(f"xpad{i}", [128, XW], F32) for i in range(NXP)]; b_xpad = [Buf(f"xpad{i}") for i in range(NXP)]
            acc = [sb(f"acc{i}", [128, TT], F32) for i in range(NAC)]; b_acc = [Buf(f"acc{i}") for i in range(NAC)]
            zo = [sb(f"zo{i}", [128, TT], F32) for i in range(NZO)]; b_zo = [Buf(f"zo{i}") for i in range(NZO)]
            xcb = [sb(f"xcb{i}", [128, TT], BF16) for i in range(NXC)]; b_xcb = [Buf(f"xcb{i}") for i in range(NXC)]
            xtm = sb("xtm", [128, NB, DI], BF16); b_xtm = Buf("xtm")
            btm = sb("btm", [128, NB, NG * NS], BF16); b_btm = Buf("btm")
            dts = sb("dts", [128, NB, NH], F32); b_dts = Buf("dts")
            carry = sb("carry", [128, 24, 3], F32); b_carry = Buf("carry")
            ps_stat = st.enter_context(nc.psum_tensor("mi_stat", [128, TT], F32)); b_stat = Buf("stat", True)
            ps_x = [st.enter_context(nc.psum_tensor(f"mi_x{i}", [128, TT], F32)) for i in range(4)]
            b_psx = [Buf(f"psx{i}", True) for i in range(4)]
            ps_tr = [st.enter_context(nc.psum_tensor(f"mi_tr{i}", [128, 2 * NB, 128], BF16)) for i in range(2)]
            b_pstr = [Buf(f"pstr{i}", True) for i in range(2)]
            P.op("pool", lambda e: e.memset(carry[:], 0.0), writes=[b_carry])

            def load_h(tt):
                P.op("sp", lambda e: e.dma_start(out=h_sb[:], in_=self.tile_ap(h_in, tt)),
                     reads=Prog.rows(bd_in, tt), writes=[b_h], dma=True)

            def prologue(tt):
                u, b_u = u2[tt % 2], b_u2[tt % 2]
                self.rms_stats(h_sb[:], b_h, sq[:], b_sq, ps_stat, b_stat, rstd, b_rstd, 8, float(D))
                for k in range(8):
                    P.op("dve", lambda e, k=k: e.scalar_tensor_tensor(
                        out=u[:, k, :], in0=h_sb[:, k, :], scalar=self.vcol("ssm_norm", k), in1=rstd[:],
                        op0=ALU.mult, op1=ALU.mult), reads=[b_h, b_rstd, self.b_const], writes=[b_u[k]])
                if tt + 1 < NT:
                    load_h(tt + 1)

            def wkey(col):
                return (col // 1024) * 1024

            def mm(col, s_, tt):
                u, b_u = u2[tt % 2], b_u2[tt % 2]
                for k in range(8):
                    P.op("pe", lambda e, k=k: e.matmul(
                        ps_x[s_][:], lhsT=win[:, k, col:col + 128], rhs=u[:, k, :], start=(k == 0), stop=(k == 7)),
                        reads=[b_u[k], b_win[(k, wkey(col))]], writes=[b_psx[s_]], inc=(k == 7))

            load_h(0)
            prologue(0)
            cnt = {"x": 0, "a": 0, "p": 0, "c": 0, "z": 0, "t": 0}

            def nxt(k, m):
                v = cnt[k] % m
                cnt[k] += 1
                return v

            for tt in range(NT):
                tsl = slice(tt * TT, (tt + 1) * TT)
                units = []
                for c in range(16):
                    s_, o = nxt("x", 4), nxt("z", NZO)

                    def z1(c=c, s_=s_, tt=tt):
                        mm(c * 128, s_, tt)

                    def z2(c=c, s_=s_, o=o, tsl=tsl, tt=tt):
                        P.op("act", lambda e: e.activation(out=zo[o][:], in_=ps_x[s_][:], func=AF.Silu),
                             reads=[b_psx[s_]], writes=[b_zo[o]])
                        P.op("pool", lambda e: e.dma_start(out=self.zsT[c * 128:(c + 1) * 128, tsl], in_=zo[o][:]),
                             reads=[b_zo[o]], writes=[bd["zsT"][(tt, c)]], dma=True)

                    units.append([z1, z2, None, None])
                if tt + 1 < NT:
                    units.append([(lambda tt=tt: prologue(tt + 1)), None, None, None])
                for c in range(24):
                    s_, a, xp, o = nxt("x", 4), nxt("a", NAC), nxt("p", NXP), nxt("c", NXC)
                    tr = nxt("t", 2) if c < 20 else 0

                    def x1(c=c, s_=s_, a=a, xp=xp, tt=tt):
                        mm(DI + c * 128, s_, tt)
                        P.op("pool", lambda e: e.tensor_copy(out=xpad[xp][:, 0:3], in_=carry[:, c, :]),
                             reads=[b_carry[c]], writes=[b_xpad[xp]["halo"]])
                        P.op("act", lambda e: e.copy(out=xpad[xp][:, 3:XW], in_=ps_x[s_][:]),
                             reads=[b_psx[s_]], writes=[b_xpad[xp]["body"]])
                        P.op("act", lambda e: e.activation(
                            out=acc[a][:], in_=ps_x[s_][:], func=AF.Identity, scale=self.vcol("ssm_conv_w", c * 4 + 3),
                            bias=self.vcol("ssm_conv_b", c)),
                            reads=[b_psx[s_], self.b_const], writes=[b_acc[a]])

                    def x2(c=c, a=a, xp=xp):
                        for tap in (2, 1, 0):
                            P.op("dve", lambda e, tap=tap: e.scalar_tensor_tensor(
                                out=acc[a][:], in0=xpad[xp][:, tap:tap + TT], scalar=self.vcol("ssm_conv_w", c * 4 + tap),
                                in1=acc[a][:], op0=ALU.mult, op1=ALU.add),
                                reads=[b_xpad[xp], b_acc[a], self.b_const], writes=[b_acc[a]])
                        P.op("pool", lambda e: e.tensor_copy(out=carry[:, c, :], in_=xpad[xp][:, TT:XW]),
                             reads=[b_xpad[xp]], writes=[b_carry[c]])

                    def x3(c=c, a=a, o=o, tsl=tsl, tt=tt):
                        P.op("act", lambda e: e.activation(out=acc[a][:], in_=acc[a][:], func=AF.Silu),
                             reads=[b_acc[a]], writes=[b_acc[a]])
                        P.op("dve", lambda e: e.tensor_copy(out=xcb[o][:], in_=acc[a][:]),
                             reads=[b_acc[a]], writes=[b_xcb[o]])
                        if c >= 16:
                            dst, nm = (self.BT, "BT") if c < 20 else (self.CT, "CT")
                            cc = (c - 16) % 4
                            P.op("pool", lambda e: e.dma_start(out=dst[cc * 128:(cc + 1) * 128, tsl], in_=xcb[o][:]),
                                 reads=[b_xcb[o]], writes=[bd[nm][(tt, cc)]], dma=True)

                    def x4(c=c, o=o, tr=tr):
                        for blk in range(NB):
                            P.op("pe", lambda e, blk=blk: e.transpose(
                                out=ps_tr[tr][:, blk, :], in_=xcb[o][:, blk * 128:(blk + 1) * 128], identity=self.cb[:, 0, :]),
                                reads=[b_xcb[o], self.b_const], writes=[b_pstr[tr]], inc=(blk == NB - 1))
                        if c < 16:
                            P.op("dve", lambda e: e.tensor_copy(out=xtm[:, :, c * 128:(c + 1) * 128], in_=ps_tr[tr][:, 0:NB, :]),
                                 reads=[b_pstr[tr]], writes=[b_xtm[c]])
                        else:
                            cc = c - 16
                            P.op("dve", lambda e: e.tensor_copy(out=btm[:, :, cc * 128:(cc + 1) * 128], in_=ps_tr[tr][:, 0:NB, :]),
                                 reads=[b_pstr[tr]], writes=[b_btm[cc]])

                    units.append([x1, x2, x3, x4 if c < 20 else None])
                s_ = nxt("x", 4)
                dtv = dts[:].rearrange("p b h -> p (b h)")

                def d1(s_=s_, tt=tt):
                    u, b_u = u2[tt % 2], b_u2[tt % 2]
                    for blk in range(NB):
                        for k in range(8):
                            P.op("pe", lambda e, k=k, blk=blk: e.matmul(
                                ps_x[s_][:, blk * NH:(blk + 1) * NH], lhsT=u[:, k, blk * 128:(blk + 1) * 128],
                                rhs=win[:, k, DI + CONVD:INP], start=(k == 0), stop=(k == 7)),
                                reads=[b_u[k], b_win[(k, wkey(DI + CONVD))]], writes=[b_psx[s_]], inc=(k == 7 and blk == NB - 1))

                def d2(s_=s_):
                    P.op("dve", lambda e: e.tensor_tensor(
                        out=dts[:], in0=ps_x[s_][:, 0:NB * NH].rearrange("p (b h) -> p b h", b=NB),
                        in1=self.vecs[:, VL.sl("dt_bias")].unsqueeze(1).to_broadcast([128, NB, NH]), op=ALU.add),
                        reads=[b_psx[s_], self.b_const], writes=[b_dts])

                def d3():
                    P.op("act", lambda e: e.activation(out=dtv, in_=dtv, func=AF.Exp), reads=[b_dts], writes=[b_dts])

                def d4(tt=tt):
                    P.op("act", lambda e: e.activation(out=dtv, in_=dtv, func=AF.Ln, bias=self.vcol("one")),
                         reads=[b_dts, self.b_const], writes=[b_dts])

                units.append([d1, d2, d3, d4])
                run_pipeline(units, 4)
                tsl_tm = lambda ap, tt=tt: ap[tt * TT:(tt + 1) * TT, :].rearrange("(b p) f -> p b f", p=128)
                P.op("pool", lambda e, f=tsl_tm: e.dma_start(out=f(self.dt_tm), in_=dts[:]),
                     reads=[b_dts], writes=[bd["dt_tm"][tt]], dma=True)
                P.op("pool", lambda e, f=tsl_tm: e.dma_start(out=f(self.xs_tm), in_=xtm[:]),
                     reads=[b_xtm], writes=[bd["xs_tm"][tt]], dma=True)
                P.op("pool", lambda e, f=tsl_tm: e.dma_start(out=f(self.B_tm), in_=btm[:]),
                     reads=[b_btm], writes=[bd["B_tm"][tt]], dma=True)

    def phase_ssd(self):
        nc, P, T = self.nc, self.P, self.T
        NC = T // 128
        self.yT = self.dram("yT", [DI, T], F32)
        bd = {n: self.dbuf(n) for n in ("xs_tm", "BT", "CT", "B_tm", "dt_tm", "yT")}
        with ExitStack() as st:
            sb = lambda n, s, d: st.enter_context(nc.sbuf_tensor(f"sd_{n}", s, d))
            dbl = lambda n, s, d: [sb(f"{n}{i}", s, d) for i in range(2)]
            bufs = lambda n: [Buf(f"{n}{i}") for i in range(2)]
            tri = lambda n, s, d: [sb(f"{n}{i}", s, d) for i in range(4)]
            tbufs = lambda n: [Buf(f"{n}{i}") for i in range(4)]
            xs_c = tri("xs", [128, NH, HP], BF16); b_xs = tbufs("xs")
            bt_c = tri("bt", [128, NG * NS], BF16); b_bt = tbufs("bt")
            BTc = tri("BT", [128, NG, 128], BF16); b_BT = tbufs("BT")
            CTc = tri("CT", [128, NG, 128], BF16); b_CT = tbufs("CT")
            dt_c = tri("dt", [128, NH], F32); b_dt = tbufs("dt")
            a_sb = sb("a", [128, NH], F32); b_a = Buf("a")
            ad = dbl("ad", [128, NH], F32); b_ad = bufs("ad")
            xd = dbl("xd", [128, NH, HP], BF16); b_xd = bufs("xd")
            xdd = dbl("xdd", [128, NH, HP], BF16); b_xdd = bufs("xdd")
            ex2 = dbl("ex2", [128, 2 * NH], F32); b_ex2 = bufs("ex2")
            seg = dbl("seg", [128, NH, 128], BF16); b_seg = bufs("seg")
            CBm = dbl("CBm", [128, 128], F32); b_CBm = bufs("CBm")
            E = dbl("E", [128, 8, 128], F32); b_E = bufs("E")
            EA = dbl("EA", [128, 8, 128], F32); b_EA = bufs("EA")
            MT = dbl("MT", [128, 8, 128], BF16); b_MT = bufs("MT")
            CsT = dbl("CsT", [128, 8, 128], BF16); b_CsT = bufs("CsT")
            S = sb("S", [128, NH, HP], F32); b_S = Buf("S")
            prev = dbl("prev", [128, NH, HP], BF16); b_prev = bufs("prev")
            yo = dbl("yo", [128, 16, 128], F32); b_yo = bufs("yo")
            pst = lambda n, s, d=F32: st.enter_context(nc.psum_tensor(f"sd_{n}", s, d))
            Dps = pst("D", [128, 2, 512]); b_D = Buf("D", True)
            Aps = pst("A", [128, 2, 512]); b_A = Buf("A", True)
            misc = pst("misc", [128, 512]); b_misc = Buf("misc", True)
            yps = [pst(f"y{i}", [128, 4, 128]) for i in range(2)]; b_yps = [Buf(f"yps{i}", True) for i in range(2)]
            stps = pst("st", [128, 8, HP]); b_st = Buf("st", True)
            TRI, SUP, ONE = self.cf[:, 1, :], self.cf[:, 2, :], self.cf[:, 3, :]
            P.op("act", lambda e: e.activation(out=a_sb[:], in_=self.vecs[:, VL.sl("a_log")], func=AF.Exp),
                 reads=[self.b_const], writes=[b_a])
            P.op("dve", lambda e: e.tensor_scalar(out=a_sb[:], in0=a_sb[:], scalar1=-1.0, scalar2=None, op0=ALU.mult),
                 reads=[b_a], writes=[b_a])
            dh = sb("dh", [128, NH], BF16); dl = sb("dl", [128, NH], BF16); dr = sb("dr", [128, NH], F32); b_dv = Buf("dv")
            Dhi = sb("Dhi", [128, NH, 128], BF16); Dlo = sb("Dlo", [128, NH, 128], BF16); b_Dm = Buf("Dm")
            dbc = self.vecs[:, VL.sl("ssm_d_bc")]
            P.op("dve", lambda e: e.tensor_copy(out=dh[:], in_=dbc), reads=[self.b_const], writes=[b_dv["h"]])
            P.op("dve", lambda e: e.tensor_tensor(out=dr[:], in0=dbc, in1=dh[:], op=ALU.subtract),
                 reads=[self.b_const, b_dv["h"]], writes=[b_dv["r"]])
            P.op("dve", lambda e: e.tensor_copy(out=dl[:], in_=dr[:]), reads=[b_dv["r"]], writes=[b_dv["l"]])
            for dst, src, key in ((Dhi, dh, "hi"), (Dlo, dl, "lo")):
                P.op("dve", lambda e, dst=dst, src=src: e.tensor_tensor(
                    out=dst[:], in0=self.cf[:, 0, :].unsqueeze(1).to_broadcast([128, NH, 128]),
                    in1=src[:].unsqueeze(2).to_broadcast([128, NH, 128]), op=ALU.mult),
                    reads=[b_dv, self.b_const], writes=[b_Dm[key]])
            P.op("pool", lambda e: e.memset(S[:], 0.0), writes=[b_S])
            P.op("pool", lambda e: e.memset(prev[0][:], 0.0), writes=[b_prev[0]])

            def load(c):
                l3 = c % 4
                tok = slice(c * 128, (c + 1) * 128)
                tt = c // (TT // 128)
                P.op("sp", lambda e: e.dma_start(out=xs_c[l3][:].rearrange("p h q -> p (h q)"), in_=self.xs_tm[tok, :]),
                     reads=[bd["xs_tm"][tt]], writes=[b_xs[l3]], dma=True)
                P.op("sp", lambda e: e.dma_start(out=bt_c[l3][:], in_=self.B_tm[tok, :]),
                     reads=[bd["B_tm"][tt]], writes=[b_bt[l3]], dma=True)
                P.op("sp", lambda e: e.dma_start(out=BTc[l3][:], in_=self.BT.rearrange("(g n) t -> n g t", n=128)[:, :, tok]),
                     reads=Prog.rows(bd["BT"], tt, 4), writes=[b_BT[l3]], dma=True)
                P.op("sp", lambda e: e.dma_start(out=CTc[l3][:], in_=self.CT.rearrange("(g n) t -> n g t", n=128)[:, :, tok]),
                     reads=Prog.rows(bd["CT"], tt, 4), writes=[b_CT[l3]], dma=True)
                P.op("sp", lambda e: e.dma_start(out=dt_c[l3][:], in_=self.dt_tm[tok, :]),
                     reads=[bd["dt_tm"][tt]], writes=[b_dt[l3]], dma=True)

            adh = dbl("adh", [128, NH], BF16); adl = dbl("adl", [128, NH], BF16); adr = dbl("adr", [128, NH], F32)
            segl = dbl("segl", [128, NH, 128], BF16); b_segl = bufs("segl")
            b_sp_ = bufs("adsplit")
            TRIb, SUPb, ONEb = self.cb[:, 1, :], self.cb[:, 2, :], self.cb[:, 3, :]
            load(0)
            if NC > 1:
                load(1)
            units = []
            pres = []
            presB = []
            gi = 0
            for c in range(NC):
                i = c % 2
                cur, nxt = c % 2, (c + 1) % 2

                def pre(c=c, i=i, l3=c % 4):
                    if c + 2 < NC:
                        load(c + 2)
                    P.op("dve", lambda e: e.tensor_tensor(out=ad[i][:], in0=dt_c[l3][:], in1=a_sb[:], op=ALU.mult),
                         reads=[b_dt[l3], b_a], writes=[b_ad[i]])
                    P.op("dve", lambda e: e.tensor_copy(out=adh[i][:], in_=ad[i][:]), reads=[b_ad[i]], writes=[b_sp_[i]["h"]])
                    P.op("dve", lambda e: e.tensor_tensor(out=adr[i][:], in0=ad[i][:], in1=adh[i][:], op=ALU.subtract),
                         reads=[b_ad[i], b_sp_[i]["h"]], writes=[b_sp_[i]["r"]])
                    P.op("dve", lambda e: e.tensor_copy(out=adl[i][:], in_=adr[i][:]), reads=[b_sp_[i]["r"]], writes=[b_sp_[i]["l"]])
                    P.op("dve", lambda e: e.tensor_tensor(
                        out=seg[i][:], in0=TRIb.unsqueeze(1).to_broadcast([128, NH, 128]),
                        in1=adh[i][:].unsqueeze(2).to_broadcast([128, NH, 128]), op=ALU.mult),
                        reads=[b_sp_[i]["h"], self.b_const], writes=[b_seg[i]])
                    P.op("dve", lambda e: e.tensor_tensor(
                        out=segl[i][:], in0=TRIb.unsqueeze(1).to_broadcast([128, NH, 128]),
                        in1=adl[i][:].unsqueeze(2).to_broadcast([128, NH, 128]), op=ALU.mult),
                        reads=[b_sp_[i]["l"], self.b_const], writes=[b_segl[i]])
                    P.op("dve", lambda e: e.tensor_tensor(
                        out=xd[i][:], in0=xs_c[l3][:], in1=dt_c[l3][:].unsqueeze(2).to_broadcast([128, NH, HP]), op=ALU.mult),
                        reads=[b_xs[l3], b_dt[l3]], writes=[b_xd[i]])

                def preB(c=c, i=i, l3=c % 4):
                    P.op("pe", lambda e: e.matmul(misc[:, 256:256 + NH], lhsT=SUP, rhs=ad[i][:], start=True, stop=True),
                         reads=[b_ad[i], self.b_const], writes=[b_misc], inc=False)
                    P.op("pe", lambda e: e.matmul(misc[:, 256 + NH:256 + 2 * NH], lhsT=ONE, rhs=ad[i][:], start=True, stop=True),
                         reads=[b_ad[i], self.b_const], writes=[b_misc])
                    P.op("act", lambda e: e.activation(out=ex2[i][:], in_=misc[:, 256:256 + 2 * NH], func=AF.Exp),
                         reads=[b_misc], writes=[b_ex2[i]])
                    P.op("dve", lambda e: e.tensor_tensor(out=adr[i][:], in0=dt_c[l3][:], in1=ex2[i][:, 0:NH], op=ALU.mult),
                         reads=[b_dt[l3], b_ex2[i], b_sp_[i]["l"]], writes=[b_sp_[i]["r"]])
                    P.op("dve", lambda e: e.tensor_tensor(
                        out=xdd[i][:], in0=xs_c[l3][:], in1=adr[i][:].unsqueeze(2).to_broadcast([128, NH, HP]), op=ALU.mult),
                        reads=[b_xs[l3], b_sp_[i]["r"]], writes=[b_xdd[i]])


                presB.append(preB)
                pres.append(pre)
                for g in range(NG):
                    j = gi % 2
                    gi += 1

                    def u1(c=c, i=i, g=g, j=j, l3=c % 4):
                        if c == 0 and g == 0:
                            pres[0]()
                            presB[0]()
                        if g == 1 and c + 1 < NC:
                            pres[c + 1]()
                        if g == 3 and c + 1 < NC:
                            presB[c + 1]()
                        P.op("pe", lambda e: e.matmul(misc[:, j * 128:(j + 1) * 128], lhsT=BTc[l3][:, g, :],
                                                      rhs=CTc[l3][:, g, :], start=True, stop=True),
                             reads=[b_BT[l3], b_CT[l3]], writes=[b_misc])
                        P.op("dve", lambda e: e.tensor_tensor(out=CBm[j][:], in0=misc[:, j * 128:(j + 1) * 128], in1=TRI, op=ALU.mult),
                             reads=[b_misc, self.b_const], writes=[b_CBm[j]])
                        for ps_, b_ps, lhs in ((Dps, b_D, SUPb), (Aps, b_A, ONEb)):
                            for half in range(2):
                                hs = slice(g * 8 + half * 4, g * 8 + half * 4 + 4)
                                P.op("pe", lambda e, ps_=ps_, lhs=lhs, half=half, hs=hs: e.matmul(
                                    ps_[:, half, :], lhsT=lhs, rhs=seg[i][:, hs, :].rearrange("p h l -> p (h l)"),
                                    start=True, stop=False),
                                    reads=[b_seg[i], self.b_const], writes=[b_ps], inc=False)
                                P.op("pe", lambda e, ps_=ps_, lhs=lhs, half=half, hs=hs: e.matmul(
                                    ps_[:, half, :], lhsT=lhs, rhs=segl[i][:, hs, :].rearrange("p h l -> p (h l)"),
                                    start=False, stop=True),
                                    reads=[b_segl[i], self.b_const], writes=[b_ps], inc=(half == 1))

                    def u2(c=c, i=i, g=g, j=j, l3=c % 4):
                        P.op("act", lambda e: e.activation(out=E[j][:].rearrange("p h l -> p (h l)"),
                                                           in_=Dps[:].rearrange("p a b -> p (a b)"), func=AF.Exp),
                             reads=[b_D], writes=[b_E[j]])
                        P.op("act", lambda e: e.activation(out=EA[j][:].rearrange("p h l -> p (h l)"),
                                                           in_=Aps[:].rearrange("p a b -> p (a b)"), func=AF.Exp),
                             reads=[b_A], writes=[b_EA[j]])
                        P.op("dve", lambda e: e.tensor_tensor(
                            out=MT[j][:], in0=E[j][:], in1=CBm[j][:].unsqueeze(1).to_broadcast([128, 8, 128]), op=ALU.mult),
                            reads=[b_E[j], b_CBm[j]], writes=[b_MT[j]])
                        P.op("dve", lambda e: e.tensor_tensor(
                            out=CsT[j][:], in0=EA[j][:], in1=CTc[l3][:, g, :].unsqueeze(1).to_broadcast([128, 8, 128]), op=ALU.mult),
                            reads=[b_EA[j], b_CT[l3]], writes=[b_CsT[j]])

                    def u3(c=c, i=i, g=g, j=j, cur=cur, nxt=nxt, last=(g == NG - 1), l3=c % 4):
                        for hh in range(8):
                            h = g * 8 + hh
                            slot, half = hh // 2, hh % 2
                            tp = (0, 64) if half else None
                            P.op("pe", lambda e, h=h, hh=hh, slot=slot, half=half, tp=tp: e.matmul(
                                yps[j][half * 64:(half + 1) * 64, slot, :], lhsT=xd[i][:, h, :], rhs=MT[j][:, hh, :],
                                start=True, stop=False, tile_position=tp),
                                reads=[b_xd[i], b_MT[j]], writes=[b_yps[j]], inc=False)
                            P.op("pe", lambda e, h=h, hh=hh, slot=slot, half=half, tp=tp: e.matmul(
                                yps[j][half * 64:(half + 1) * 64, slot, :], lhsT=prev[cur][:, h, :], rhs=CsT[j][:, hh, :],
                                start=False, stop=False, tile_position=tp),
                                reads=[b_prev[cur][g], b_CsT[j]], writes=[b_yps[j]], inc=False)
                            for Dm, lastd in ((Dhi, False), (Dlo, True)):
                                P.op("pe", lambda e, h=h, slot=slot, half=half, tp=tp, Dm=Dm, lastd=lastd: e.matmul(
                                    yps[j][half * 64:(half + 1) * 64, slot, :], lhsT=xs_c[l3][:, h, :], rhs=Dm[:, h, :],
                                    start=False, stop=lastd, tile_position=tp),
                                    reads=[b_xs[l3], b_Dm], writes=[b_yps[j]], inc=(lastd and hh == 7))
                        P.op("act", lambda e: e.copy(out=yo[i][:, g * 4:(g + 1) * 4, :], in_=yps[j][:]),
                             reads=[b_yps[j]], writes=[b_yo[i][g]])
                        P.op("pe", lambda e: e.matmul(
                            stps[:].rearrange("p h q -> p (h q)"), lhsT=bt_c[l3][:, g * 128:(g + 1) * 128],
                            rhs=xdd[i][:, g * 8:(g + 1) * 8, :].rearrange("p h q -> p (h q)"), start=True, stop=True),
                            reads=[b_bt[l3], b_xdd[i]], writes=[b_st])
                        P.op("dve", lambda e: e.tensor_tensor(
                            out=S[:, g * 8:(g + 1) * 8, :], in0=S[:, g * 8:(g + 1) * 8, :],
                            in1=ex2[i][:, NH + g * 8:NH + (g + 1) * 8].unsqueeze(2).to_broadcast([128, 8, HP]), op=ALU.mult),
                            reads=[b_S[g], b_ex2[i]], writes=[b_S[g]])
                        P.op("dve", lambda e: e.tensor_tensor(
                            out=S[:, g * 8:(g + 1) * 8, :], in0=S[:, g * 8:(g + 1) * 8, :], in1=stps[:], op=ALU.add),
                            reads=[b_S[g], b_st], writes=[b_S[g]])
                        P.op("act", lambda e: e.copy(out=prev[nxt][:, g * 8:(g + 1) * 8, :], in_=S[:, g * 8:(g + 1) * 8, :]),
                             reads=[b_S[g]], writes=[b_prev[nxt][g]])
                        if last:
                            tt = c // (TT // 128)
                            P.op("pool", lambda e: e.dma_start(
                                out=self.yT.rearrange("(q p) t -> p q t", p=128)[:, :, c * 128:(c + 1) * 128], in_=yo[i][:]),
                                reads=[b_yo[i]], writes=[bd["yT"][(tt, c % 4)]], dma=True)

                    units.append([u1, u2, u3])
            run_pipeline(units, 3, order=(1, 2, 0))

    def residual_out(self, tt, ms, wt, b_w, NK, rhs, b_rhs, ps_o, b_pso, hres, b_hres, hout, b_hout, h_in, h_out):
        P = self.P
        bd_in, bd_out = self.dbuf(h_in), self.dbuf(h_out)
        for m in ms:
            o = m % 2
            for kk in range(NK):
                P.op("pe", lambda e, kk=kk, m=m, o=o: e.matmul(
                    ps_o[o][:], lhsT=wt[:, kk, m * 128:(m + 1) * 128], rhs=rhs[:, kk, :],
                    start=(kk == 0), stop=(kk == NK - 1)),
                    reads=[b_rhs[kk], b_w[(kk, 0)]], writes=[b_pso[o]], inc=(kk == NK - 1))
            P.op("sp", lambda e, m=m, o=o: e.dma_start(
                out=hres[o][:], in_=getattr(self, h_in)[m * 128:(m + 1) * 128, tt * TT:(tt + 1) * TT]),
                reads=[bd_in[(tt, m)]], writes=[b_hres[o]], dma=True)
            P.op("dve", lambda e, o=o: e.tensor_tensor(
                out=hout[o][:], in0=hres[o][:], in1=ps_o[o][:], op=ALU.add),
                reads=[b_hres[o], b_pso[o]], writes=[b_hout[o]])
            P.op("pool", lambda e, m=m, o=o: e.dma_start(
                out=getattr(self, h_out)[m * 128:(m + 1) * 128, tt * TT:(tt + 1) * TT], in_=hout[o][:]),
                reads=[b_hout[o]], writes=[bd_out[(tt, m)]], dma=True)

    def phase_m_out(self, h_in, h_out):
        nc, P, NT, T = self.nc, self.P, self.NT, self.T
        bd = {n: self.dbuf(n) for n in ("zsT", "yT")}
        with ExitStack() as st:
            sb = lambda n, s, d: st.enter_context(nc.sbuf_tensor(f"mo_{n}", s, d))
            dbl = lambda n, s, d: [sb(f"{n}{i}", s, d) for i in range(2)]
            bufs = lambda n: [Buf(f"{n}{i}") for i in range(2)]
            wo = sb("wo", [128, 16, D], BF16); b_wo = Buf("wo")
            self.load_weight_bf16(wo, b_wo, self.w["ssm_out_w"], 16, D)
            yl = [sb(f"y{i}", [128, 4, TT], F32) for i in range(4)]; b_yl = [Buf(f"y{i}") for i in range(4)]
            zl = [sb(f"z{i}", [128, 4, TT], F32) for i in range(4)]; b_zl = [Buf(f"z{i}") for i in range(4)]
            sq2 = dbl("sq", [128, 4, TT], BF16); b_sq2 = bufs("sq")
            rstd2 = dbl("rstd", [128, TT], F32); b_rstd2 = bufs("rstd")
            yn2 = [sb(f"yn{i}", [128, 16, TT], BF16) for i in range(2)]; b_yn2 = [Buf(f"yn{i}") for i in range(2)]
            hres = dbl("hres", [128, TT], F32); b_hres = bufs("hres")
            hout = dbl("hout", [128, TT], F32); b_hout = bufs("hout")
            ps_stat2 = [st.enter_context(nc.psum_tensor(f"mo_stat{i}", [128, TT], F32)) for i in range(2)]
            b_stat2 = [Buf(f"stat{i}", True) for i in range(2)]
            ps_o = [st.enter_context(nc.psum_tensor(f"mo_o{i}", [128, TT], F32)) for i in range(2)]
            b_pso = [Buf(f"pso{i}", True) for i in range(2)]
            units = []
            for tt in range(NT + 1):
                for g in range(NG):
                    n_ = tt * NG + g
                    i4, i = n_ % 4, n_ % 2
                    if tt == NT:
                        units.append([None, None, None, None,
                                      (lambda tt=tt, g=g: self.residual_out(tt - 1, range(2 * g, 2 * g + 2), wo, b_wo, 16,
                                                                            yn2[(tt - 1) % 2], b_yn2[(tt - 1) % 2], ps_o, b_pso,
                                                                            hres, b_hres, hout, b_hout, h_in, h_out))])
                        continue
                    yn, b_yn = yn2[tt % 2], b_yn2[tt % 2]
                    tsl = slice(tt * TT, (tt + 1) * TT)

                    def g0(g=g, i4=i4, tt=tt, tsl=tsl):
                        for nm, dst, bb in (("yT", yl, b_yl), ("zsT", zl, b_zl)):
                            src = getattr(self, nm)[g * 512:(g + 1) * 512, tsl].rearrange("(k p) t -> p k t", p=128)
                            if nm == "yT":
                                rd = Prog.rows(bd[nm], tt, 4)
                            else:
                                rd = [bd[nm][(tt, g * 4 + k)] for k in range(4)]
                            P.op("sp", lambda e, src=src, d=dst[i4]: e.dma_start(out=d[:], in_=src),
                                 reads=rd, writes=[bb[i4]], dma=True)

                    def g1(i4=i4):
                        for k in range(4):
                            P.op("dve", lambda e, k=k: e.tensor_tensor(
                                out=yl[i4][:, k, :], in0=yl[i4][:, k, :], in1=zl[i4][:, k, :], op=ALU.mult),
                                reads=[b_yl[i4], b_zl[i4]], writes=[b_yl[i4][k]])

                    def g2(i=i, i4=i4):
                        self.rms_stats(yl[i4][:], b_yl[i4], sq2[i][:], [b_sq2[i]], ps_stat2[i], b_stat2[i], rstd2[i], b_rstd2[i], 4, 512.0)

                    def g3(g=g, i=i, i4=i4, yn=yn, b_yn=b_yn):
                        for k in range(4):
                            c = g * 4 + k
                            P.op("dve", lambda e, k=k, c=c: e.scalar_tensor_tensor(
                                out=yn[:, c, :], in0=yl[i4][:, k, :], scalar=self.vcol("gate_norm", c), in1=rstd2[i][:],
                                op0=ALU.mult, op1=ALU.mult), reads=[b_yl[i4], b_rstd2[i], self.b_const], writes=[b_yn[c]])

                    r4 = None
                    if tt >= 1:
                        r4 = (lambda tt=tt, g=g: self.residual_out(tt - 1, range(2 * g, 2 * g + 2), wo, b_wo, 16,
                                                                   yn2[(tt - 1) % 2], b_yn2[(tt - 1) % 2], ps_o, b_pso,
                                                                   hres, b_hres, hout, b_hout, h_in, h_out))
                    units.append([g0, g1, g2, g3, r4])
            run_pipeline(units, 5, oldest_first=True)

    def phase_a_qkv(self, h_in):
        nc, P, NT, T = self.nc, self.P, self.NT, self.T
        bd_in = self.dbuf(h_in)
        for n, shp in [("KT", [D, T]), ("QT", [D, T]), ("V_tm", [T, D])]:
            setattr(self, n, self.dram(n, shp, BF16))
        bd = {n: self.dbuf(n) for n in ("KT", "QT", "V_tm")}
        NB = TT // 128
        with ExitStack() as st:
            sb = lambda n, s, d: st.enter_context(nc.sbuf_tensor(f"aq_{n}", s, d))
            wk = sb("wk", [128, 8, D], BF16); b_wk = Buf("wk")
            wv = sb("wv", [128, 8, D], BF16); b_wv = Buf("wv")
            wq = sb("wq", [128, 8, D], BF16); b_wq = Buf("wq")
            self.load_weight_bf16(wk, b_wk, self.w["w_k"], 8, D)
            self.load_weight_bf16(wv, b_wv, self.w["w_v"], 8, D)
            self.load_weight_bf16(wq, b_wq, self.w["w_q"], 8, D)
            h_sb = [sb(f"h{i}", [128, 8, TT], F32) for i in range(2)]; b_h = [Buf(f"h{i}") for i in range(2)]
            sq = sb("sq", [128, 8, TT], BF16); b_sq = Buf("sq")
            rstd = sb("rstd", [128, TT], F32); b_rstd = Buf("rstd")
            uk2 = [sb(f"uk{i}", [128, 8, TT], BF16) for i in range(2)]; b_uk2 = [Buf(f"uk{i}") for i in range(2)]
            uq2 = [sb(f"uq{i}", [128, 8, TT], BF16) for i in range(2)]; b_uq2 = [Buf(f"uq{i}") for i in range(2)]
            ob = [sb(f"ob{i}", [128, TT], BF16) for i in range(2)]; b_ob = [Buf(f"ob{i}") for i in range(2)]
            vb = [sb(f"vb{i}", [128, D], BF16) for i in range(2)]; b_vb = [Buf(f"vb{i}") for i in range(2)]
            ps_stat = st.enter_context(nc.psum_tensor("aq_stat", [128, TT], F32)); b_stat = Buf("stat", True)
            ps_x = [st.enter_context(nc.psum_tensor(f"aq_x{i}", [128, TT], F32)) for i in range(4)]
            b_psx = [Buf(f"psx{i}", True) for i in range(4)]

            def load_h(tt):
                i = tt % 2
                P.op("sp", lambda e: e.dma_start(out=h_sb[i][:], in_=self.tile_ap(h_in, tt)),
                     reads=Prog.rows(bd_in, tt), writes=[b_h[i]], dma=True)

            def prologue(tt):
                i = tt % 2
                uk, uq, b_uk, b_uq = uk2[i], uq2[i], b_uk2[i], b_uq2[i]
                self.rms_stats(h_sb[i][:], b_h[i], sq[:], [b_sq], ps_stat, b_stat, rstd, b_rstd, 8, float(D))
                for k in range(8):
                    P.op("dve", lambda e, k=k: e.scalar_tensor_tensor(
                        out=uk[:, k, :], in0=h_sb[i][:, k, :], scalar=self.vcol("kv_norm", k), in1=rstd[:],
                        op0=ALU.mult, op1=ALU.mult), reads=[b_h[i], b_rstd, self.b_const], writes=[b_uk[k]])
                    P.op("dve", lambda e, k=k: e.scalar_tensor_tensor(
                        out=uq[:, k, :], in0=h_sb[i][:, k, :], scalar=self.vcol("attn_norm", k), in1=rstd[:],
                        op0=ALU.mult, op1=ALU.mult), reads=[b_h[i], b_rstd, self.b_const], writes=[b_uq[k]])
                if tt + 2 < NT:
                    load_h(tt + 2)

            load_h(0)
            if NT > 1:
                load_h(1)
            prologue(0)
            xi = 0
            oi = 0
            for tt in range(NT):
                i = tt % 2
                uk, uq, b_uk, b_uq = uk2[i], uq2[i], b_uk2[i], b_uq2[i]
                tsl = slice(tt * TT, (tt + 1) * TT)
                for which in range(2):
                    wt, b_w, uu, b_uu, dst, nm, scl = ((wk, b_wk, uk, b_uk, self.KT, "KT", 1.0),
                                                      (wq, b_wq, uq, b_uq, self.QT, "QT", 0.125))[which]
                    for m in range(8):
                        s_ = xi % 4
                        xi += 1
                        for k in range(8):
                            P.op("pe", lambda e, k=k, m=m, s_=s_, wt=wt, uu=uu: e.matmul(
                                ps_x[s_][:], lhsT=wt[:, k, m * 128:(m + 1) * 128], rhs=uu[:, k, :],
                                start=(k == 0), stop=(k == 7)),
                                reads=[b_uu[k], b_w[(k, 0)]], writes=[b_psx[s_]], inc=(k == 7))
                        o = oi % 2
                        oi += 1
                        P.op("act", lambda e, s_=s_, o=o, scl=scl: e.activation(out=ob[o][:], in_=ps_x[s_][:], func=AF.Copy, scale=scl),
                             reads=[b_psx[s_]], writes=[b_ob[o]])
                        P.op("pool", lambda e, m=m, o=o, dst=dst, tsl=tsl: e.dma_start(out=dst[m * 128:(m + 1) * 128, tsl], in_=ob[o][:]),
                             reads=[b_ob[o]], writes=[bd[nm][(tt, m)]], dma=True)
                    if which == 0 and tt + 1 < NT:
                        prologue(tt + 1)
                for blk in range(NB):
                    o = blk % 2
                    for half in range(2):
                        s_ = xi % 4
                        xi += 1
                        for k in range(8):
                            P.op("pe", lambda e, k=k, blk=blk, half=half, s_=s_, uk=uk: e.matmul(
                                ps_x[s_][:], lhsT=uk[:, k, blk * 128:(blk + 1) * 128], rhs=wv[:, k, half * 512:(half + 1) * 512],
                                start=(k == 0), stop=(k == 7)),
                                reads=[b_uk[k], b_wv[(k, 0)]], writes=[b_psx[s_]], inc=(k == 7))
                        P.op("act", lambda e, s_=s_, o=o, half=half: e.copy(out=vb[o][:, half * 512:(half + 1) * 512], in_=ps_x[s_][:]),
                             reads=[b_psx[s_]], writes=[b_vb[o][half]])
                    r0 = tt * TT + blk * 128
                    P.op("pool", lambda e, o=o, r0=r0: e.dma_start(out=self.V_tm[r0:r0 + 128, :], in_=vb[o][:]),
                         reads=[b_vb[o]], writes=[bd["V_tm"][(tt, blk)]], dma=True)

    def phase_attn(self):
        nc, P, T = self.nc, self.P, self.T
        NQG = T // TT
        NKB = T // 128
        self.oT = self.dram("oT", [D, T], BF16)
        bd = {n: self.dbuf(n) for n in ("KT", "QT", "V_tm", "oT")}
        with ExitStack() as st:
            sb = lambda n, s, d: st.enter_context(nc.sbuf_tensor(f"at_{n}", s, d))
            ring = lambda n, s, d, k: [sb(f"{n}{i}", s, d) for i in range(k)]
            bufs = lambda n, k: [Buf(f"{n}{i}") for i in range(k)]
            KTp = ring("KT", [128, T], BF16, 2); b_KT = bufs("KT", 2)
            QTp = ring("QT", [128, T], BF16, 2); b_QT = bufs("QT", 2)
            Vp = ring("V", [128, NKB, 128], BF16, 2); b_V = bufs("V", 2)
            NE, NS_, NA = 4, 5, 3
            ez = ring("ez", [128, 2, TT], F32, NE); b_ez = bufs("ez", NE)
            sp = ring("sp", [128, 2, TT], BF16, NS_); b_sp = bufs("sp", NS_)
            pm = ring("pm", [128, 2, TT], F32, 2); b_pm = bufs("pm", 2)
            att = ring("att", [128, 2, TT], BF16, NA); b_att = bufs("att", NA)
            osb = ring("o", [128, TT], BF16, 2); b_osb = bufs("o", 2)
            msk = sb("msk", [128, 4, TT], F32); b_msk = Buf("msk")
            P.op("sp", lambda e: e.dma_start(out=msk[:].rearrange("p a b -> p (a b)"), in_=self.consts_d[:, NCB * 128:NCONST * 128]),
                 writes=[b_msk], dma=True)
            pst = lambda n, s: st.enter_context(nc.psum_tensor(f"at_{n}", s, F32))
            zps = [pst(f"z{i}", [128, 2, TT]) for i in range(2)]; b_z = [Buf(f"z{i}", True) for i in range(2)]
            xps = pst("x", [128, 2, TT]); b_x = Buf("x", True)
            ops = [pst(f"op{i}", [128, TT]) for i in range(2)]; b_o = [Buf(f"op{i}", True) for i in range(2)]
            TGE, TLT = self.cb[:, 4, :], self.cb[:, 5, :]

            def load_pair(p):
                i = p % 2
                P.op("sp", lambda e: e.dma_start(out=KTp[i][:], in_=self.KT[p * 128:(p + 1) * 128, :]),
                     reads=[bd["KT"]], writes=[b_KT[i]], dma=True)
                P.op("sp", lambda e: e.dma_start(out=QTp[i][:], in_=self.QT[p * 128:(p + 1) * 128, :]),
                     reads=[bd["QT"]], writes=[b_QT[i]], dma=True)
                P.op("sp", lambda e: e.dma_start(
                    out=Vp[i][:], in_=self.V_tm[:, p * 128:(p + 1) * 128].rearrange("(b p) d -> p b d", p=128)),
                    reads=[bd["V_tm"]], writes=[b_V[i]], dma=True)

            steps = []
            for p in range(AH // 2):
                for qg in range(NQG):
                    kbs = list(range(qg * 4 + 3, -1, -1))
                    for n, kb in enumerate(kbs):
                        steps.append((p, qg, kb, n == 0, n == len(kbs) - 1))

            def stage1(si):
                p, qg, kb, first, last = steps[si]
                pi, r, z = p % 2, si % NE, si % 2
                for hd in range(2):
                    rows = slice(hd * 64, (hd + 1) * 64)
                    P.op("pe", lambda e, hd=hd, rows=rows: e.matmul(
                        zps[z][:, hd, :], lhsT=KTp[pi][rows, kb * 128:(kb + 1) * 128],
                        rhs=QTp[pi][rows, qg * TT:(qg + 1) * TT], start=True, stop=True),
                        reads=[b_KT[pi], b_QT[pi]], writes=[b_z[z]], inc=(hd == 1))
                j = kb - qg * 4
                c0 = max(j, 0) * 128
                P.op("act", lambda e: e.activation(out=ez[r][:, :, c0:], in_=zps[z][:, :, c0:], func=AF.Exp),
                     reads=[b_z[z]], writes=[b_ez[r]])
                if j >= 0:
                    P.op("dve", lambda e: e.tensor_tensor(out=ez[r][:, :, c0:], in0=ez[r][:, :, c0:],
                                                          in1=msk[:, j, c0:].unsqueeze(1).to_broadcast([128, 2, TT - c0]), op=ALU.mult),
                         reads=[b_ez[r], b_msk], writes=[b_ez[r]])
                rs = si % NS_
                if c0 > 0:
                    P.op("pool", lambda e: e.memset(sp[rs][:, :, 0:c0], 0.0), writes=[b_sp[rs]])
                P.op("act", lambda e: e.activation(out=sp[rs][:, :, c0:], in_=ez[r][:, :, c0:], func=AF.Ln, bias=self.vcol("one")),
                     reads=[b_ez[r], self.b_const], writes=[b_sp[rs]["v"]])

            def stage2(si):
                p, qg, kb, first, last = steps[si]
                r, rs, rp, ra = si % NE, si % NS_, (si - 1) % NS_, si % NA
                for hd in range(2):
                    P.op("pe", lambda e, hd=hd: e.matmul(xps[:, hd, :], lhsT=TGE, rhs=sp[rs][:, hd, :],
                                                         start=first, stop=first, skip_group_check=True),
                         reads=[b_sp[rs], self.b_const], writes=[b_x], inc=(first and hd == 1))
                if not first:
                    for hd in range(2):
                        P.op("pe", lambda e, hd=hd: e.matmul(xps[:, hd, :], lhsT=TLT, rhs=sp[rp][:, hd, :],
                                                             start=False, stop=True, skip_group_check=True),
                             reads=[b_sp[rp], self.b_const], writes=[b_x], inc=(hd == 1))
                pi_ = si % 2
                c0 = max(kb - qg * 4, 0) * 128
                P.op("act", lambda e: e.activation(out=pm[pi_][:, :, c0:], in_=xps[:, :, c0:], func=AF.Exp, scale=-1.0),
                     reads=[b_x], writes=[b_pm[pi_]])
                if c0 > 0:
                    P.op("pool", lambda e: e.memset(att[ra][:, :, 0:c0], 0.0), writes=[b_att[ra]])
                P.op("dve", lambda e: e.tensor_tensor(out=att[ra][:, :, c0:], in0=ez[r][:, :, c0:], in1=pm[pi_][:, :, c0:], op=ALU.mult),
                     reads=[b_ez[r], b_pm[pi_]], writes=[b_att[ra]["v"]])

            def stage3(si):
                p, qg, kb, first, last = steps[si]
                pi, ra = p % 2, si % NA
                o = (p * NQG + qg) % 2
                for hd in range(2):
                    rows = slice(hd * 64, (hd + 1) * 64)
                    P.op("pe", lambda e, hd=hd, rows=rows: e.matmul(
                        ops[o][rows, :], lhsT=Vp[pi][:, kb, rows], rhs=att[ra][:, hd, :], start=first, stop=last,
                        tile_position=((0, 64) if hd else None)),
                        reads=[b_V[pi], b_att[ra]], writes=[b_o[o]], inc=(hd == 1))
                if last and qg == NQG - 1 and p + 2 < AH // 2:
                    load_pair(p + 2)
                if last:
                    P.op("act", lambda e: e.copy(out=osb[o][:], in_=ops[o][:]), reads=[b_o[o]], writes=[b_osb[o]])
                    P.op("pool", lambda e: e.dma_start(out=self.oT[p * 128:(p + 1) * 128, qg * TT:(qg + 1) * TT], in_=osb[o][:]),
                         reads=[b_osb[o]], writes=[bd["oT"][(qg, p)]], dma=True)

            n = len(steps)
            L2, L3 = 2, 3
            load_pair(0)
            load_pair(1)
            for i in range(n + L3):
                if i < n:
                    stage1(i)
                if 0 <= i - L2 < n:
                    stage2(i - L2)
                if 0 <= i - L3 < n:
                    stage3(i - L3)

    def phase_a_out(self, h_in, h_out):
        nc, P, NT = self.nc, self.P, self.NT
        bd_o = self.dbuf("oT")
        with ExitStack() as st:
            sb = lambda n, s, d: st.enter_context(nc.sbuf_tensor(f"ao_{n}", s, d))
            wo = sb("wo", [128, 8, D], BF16); b_wo = Buf("wo")
            self.load_weight_bf16(wo, b_wo, self.w["w_o"], 8, D)
            ot = [sb(f"ot{i}", [128, 8, TT], BF16) for i in range(2)]; b_ot = [Buf(f"ot{i}") for i in range(2)]
            hres = [sb(f"hres{i}", [128, TT], F32) for i in range(2)]; b_hres = [Buf(f"hres{i}") for i in range(2)]
            hout = [sb(f"hout{i}", [128, TT], F32) for i in range(2)]; b_hout = [Buf(f"hout{i}") for i in range(2)]
            ps_o = [st.enter_context(nc.psum_tensor(f"ao_o{i}", [128, TT], F32)) for i in range(2)]
            b_pso = [Buf(f"pso{i}", True) for i in range(2)]
            for tt in range(NT):
                i = tt % 2
                P.op("sp", lambda e, i=i, tt=tt: e.dma_start(out=ot[i][:], in_=self.tile_ap("oT", tt)),
                     reads=Prog.rows(bd_o, tt), writes=[b_ot[i]], dma=True)
                self.residual_out(tt, range(8), wo, b_wo, 8, ot[i], b_ot[i], ps_o, b_pso, hres, b_hres, hout, b_hout, h_in, h_out)

    def phase_norm(self, h_in, h_out, wname, is_output):
        nc, P, NT = self.nc, self.P, self.NT
        bd_in, bd_out = self.dbuf(h_in), self.dbuf(h_out)
        with ExitStack() as st:
            sb = lambda n, s, d: st.enter_context(nc.sbuf_tensor(f"n_{n}", s, d))
            h_sb = [sb(f"h{i}", [128, 8, TT], F32) for i in range(2)]; b_h = [Buf(f"h{i}") for i in range(2)]
            sq = sb("sq", [128, 8, TT], BF16); b_sq = Buf("sq")
            rstd = sb("rstd", [128, TT], F32); b_rstd = Buf("rstd")
            ps_stat = st.enter_context(nc.psum_tensor("n_stat", [128, TT], F32)); b_stat = Buf("stat", True)
            for tt in range(NT):
                i = tt % 2
                P.op("sp", lambda e, i=i, tt=tt: e.dma_start(out=h_sb[i][:], in_=self.tile_ap(h_in, tt)),
                     reads=Prog.rows(bd_in, tt), writes=[b_h[i]], dma=True)
                self.rms_stats(h_sb[i][:], b_h[i], sq[:], [b_sq], ps_stat, b_stat, rstd, b_rstd, 8, float(D))
                for k in range(8):
                    P.op("dve", lambda e, k=k, i=i: e.scalar_tensor_tensor(
                        out=h_sb[i][:, k, :], in0=h_sb[i][:, k, :], scalar=self.vcol(wname, k), in1=rstd[:],
                        op0=ALU.mult, op1=ALU.mult), reads=[b_h[i], b_rstd, self.b_const], writes=[b_h[i][k]])
                tok = P.op("pool", lambda e, i=i, tt=tt: e.dma_start(out=self.tile_ap(h_out, tt), in_=h_sb[i][:]),
                           reads=[b_h[i]], writes=Prog.rows(bd_out, tt), dma=True)
                if is_output:
                    self.out_toks.append(tok)


WNAMES = ["ssm_in_w", "ssm_out_w", "w_k", "w_v", "w_q", "w_o", "ffn_up_w0", "ffn_up_w1", "ffn_down_w0",
          "ffn_down_w1"]


def get_weight(inp, name):
    if name in ("w_k", "w_v"):
        a = inp[name]
    elif name[-1] in "01" and name.startswith("ffn"):
        a = inp[name[:-1]][int(name[-1])]
    else:
        a = inp[name][0]
    return np.ascontiguousarray(np.asarray(a, np.float32))


FULL_PHASES = [
    ("m_in", "xT"), ("ssd",), ("m_out", "xT", "hA"), ("ffn", 0, "hA", "hB"),
    ("a_qkv", "hB"), ("attn",), ("a_out", "hB", "hA"), ("ffn", 1, "hA", "hB"),
    ("norm", "hB", "outT", "final_norm", True),
]

_CACHE = {}


def kernel(**inputs):
    x = np.asarray(inputs["x"], np.float32)
    B, T, _ = x.shape
    assert B == NCORES and T % TT == 0
    if T not in _CACHE:
        _CACHE[T] = Builder(T, FULL_PHASES).build()
    nc = _CACHE[T]
    shared = {"vecs": pack_vecs(inputs), "consts": make_consts()}
    for name in WNAMES:
        shared[name] = get_weight(inputs, name)
    in_maps = []
    for b in range(B):
        m = dict(shared)
        m["xT"] = np.ascontiguousarray(x[b].T)
        in_maps.append(m)
    res = run_bass_kernel_spmd(nc, in_maps, core_ids=list(range(NCORES)))
    out = np.stack([np.ascontiguousarray(np.asarray(r["outT"], np.float32).T) for r in res.results], axis=0)
    return out
```

```python
import math
from contextlib import ExitStack

import numpy as np
import concourse.bass as bass
import concourse.mybir as mybir
from concourse.bass_utils import run_bass_kernel_spmd
from concourse.alu_op_type import AluOpType as ALU

F32 = mybir.dt.float32
BF16 = mybir.dt.bfloat16
AF = mybir.ActivationFunctionType

D = 1024
DI = 2048
NH = 32
HP = 64
NG = 4
NS = 128
CONVD = 3072
INP = 5152
DFF = 2816
AH = 16
AD = 64
EPS = 1e-6
TT = 512
NCORES = 8
NCONST = 22
NCB = 6

ENGS = ("pe", "act", "dve", "pool", "sp")
DEBUG_LEVEL = 99
EXTRA_NOPS = 0


class Buf:
    def __init__(self, name, excl=False):
        self.name = name
        self.excl = excl
        self.w = {}
        self.r = {}
        self.regions = {}

    def __getitem__(self, key):
        return (self, key)


def _merge(dst, src):
    for k, (s, v) in src.items():
        if k not in dst or dst[k][1] < v:
            dst[k] = (s, v)


class Prog:
    SEM_LIMIT = 30000

    def __init__(self, nc, stack):
        self.nc = nc
        self.stack = stack
        self.q = {e: [] for e in ENGS}
        self.cur = {}
        self.cnt = {}
        self.waited = {e: {} for e in ENGS}
        self.pending = {e: [] for e in ENGS}
        for e in ("pe", "act", "dve", "pool"):
            self._new_sem(e)
        self.ring = {}
        self.ring_i = {}
        self.R = 8
        for e in ("sp", "pool", "act"):
            self.ring[e] = [stack.enter_context(nc.semaphore(f"dq_{e}_{i}")) for i in range(self.R)]
            self.ring_i[e] = 0
        self.nsem = 0

    def _new_sem(self, e):
        n = getattr(self, "_semn", 0)
        self._semn = n + 1
        self.cur[e] = self.stack.enter_context(self.nc.semaphore(f"s_{e}_{n}"))
        self.cnt[e] = 0

    @staticmethod
    def _split(b):
        if isinstance(b, tuple):
            return b[0], b[1]
        return b, None

    def _deps_read(self, b, deps):
        buf, key = self._split(b)
        _merge(deps, buf.w)
        if key is None:
            for w, _ in buf.regions.values():
                _merge(deps, w)
        elif key in buf.regions:
            _merge(deps, buf.regions[key][0])

    def _deps_write(self, b, deps):
        buf, key = self._split(b)
        _merge(deps, buf.w)
        _merge(deps, buf.r)
        if key is None:
            for w, r in buf.regions.values():
                _merge(deps, w)
                _merge(deps, r)
        elif key in buf.regions:
            _merge(deps, buf.regions[key][0])
            _merge(deps, buf.regions[key][1])

    def _note_read(self, b, tok):
        buf, key = self._split(b)
        if key is None:
            _merge(buf.r, tok)
        else:
            reg = buf.regions.setdefault(key, [{}, {}])
            _merge(reg[1], tok)

    def _note_write(self, b, tok):
        buf, key = self._split(b)
        if key is None:
            buf.w = dict(tok)
            buf.r = {}
            buf.regions = {}
        else:
            buf.regions[key] = [dict(tok), {}]

    @staticmethod
    def rows(buf, tt, n=8):
        return [buf[(tt, m)] for m in range(n)]

    def op(self, eng, fn, reads=(), writes=(), inc=True, dma=False):
        rdeps, wdeps = {}, {}
        xreads = [b for b in reads if self._split(b)[0].excl]
        reads = [b for b in reads if not self._split(b)[0].excl]
        for b in reads:
            self._deps_read(b, rdeps)
        for b in xreads:
            self._deps_read(b, rdeps)
            self._deps_write(b, wdeps)
        for b in writes:
            self._deps_write(b, wdeps)
        own = id(self.cur[eng]) if eng in self.cur else None
        deps = {}
        for k, (s, v) in rdeps.items():
            if k == own and eng == "pe":
                continue
            if k not in deps or deps[k][1] < v:
                deps[k] = (s, v)
        for k, (s, v) in wdeps.items():
            if k == own:
                continue
            if k not in deps or deps[k][1] < v:
                deps[k] = (s, v)
        wd = self.waited[eng]
        for k, (s, v) in deps.items():
            if wd.get(k, 0) < v:
                wd[k] = v
                self.q[eng].append(lambda e, s=s, v=v: e.wait_ge(s, v))
        if dma:
            i = self.ring_i[eng]
            self.ring_i[eng] = i + 1
            sem = self.ring[eng][i % self.R]
            val = 16 * (i // self.R + 1)
            if i >= self.R:
                prev = 16 * (i // self.R)
                if wd.get(id(sem), 0) < prev:
                    wd[id(sem)] = prev
                    self.q[eng].append(lambda e, s=sem, v=prev: e.wait_ge(s, v))
            self.q[eng].append(lambda e, s=sem: fn(e).then_inc(s, 16))
            tok = {id(sem): (sem, val)}
        else:
            if self.cnt[eng] >= self.SEM_LIMIT and not self.pending[eng]:
                self._new_sem(eng)
            sem = self.cur[eng]
            val = self.cnt[eng] + 1
            tok = {id(sem): (sem, val)}
            if inc:
                self.cnt[eng] = val
                self.q[eng].append(lambda e, s=sem: fn(e).then_inc(s, 1))
                self.pending[eng] = []
            else:
                self.q[eng].append(lambda e: fn(e))
                self.pending[eng].append(1)
        for b in reads:
            self._note_read(b, tok)
        for b in xreads:
            self._note_read(b, tok)
        for b in writes:
            self._note_write(b, tok)
        return tok

    def final_wait(self, eng, toks):
        for tok in toks:
            for k, (s, v) in tok.items():
                self.q[eng].append(lambda e, s=s, v=v: e.wait_ge(s, v))

    def barrier(self):
        assert all(not p for p in self.pending.values())
        toks = []
        for e in ("pe", "act", "dve", "pool"):
            if self.cnt[e] > 0:
                toks.append((self.cur[e], self.cnt[e]))
        for eng in self.ring:
            n = self.ring_i[eng]
            for r in range(min(n, self.R)):
                toks.append((self.ring[eng][r], 16 * ((n - 1 - r) // self.R + 1)))
        for eng in ENGS:
            wd = self.waited[eng]
            for s, v in toks:
                if wd.get(id(s), 0) < v:
                    wd[id(s)] = v
                    self.q[eng].append(lambda e, s=s, v=v: e.wait_ge(s, v))

    def drain_dma(self):
        for eng in self.ring:
            n = self.ring_i[eng]
            for r in range(min(n, self.R)):
                cnt = (n - 1 - r) // self.R + 1
                self.q[eng].append(lambda e, s=self.ring[eng][r], v=16 * cnt: e.wait_ge(s, v))

    def emit(self):
        assert all(not p for p in self.pending.values()), "dangling inc=False op"
        with self.nc.Block() as block:
            @block.tensor
            def _(e):
                for f in self.q["pe"]:
                    f(e)

            @block.scalar
            def _(e):
                for f in self.q["act"]:
                    f(e)

            @block.vector
            def _(e):
                for f in self.q["dve"]:
                    f(e)

            @block.gpsimd
            def _(e):
                for f in self.q["pool"]:
                    f(e)

            @block.sync
            def _(e):
                for f in self.q["sp"]:
                    f(e)


def run_pipeline(units, nstage, oldest_first=False, order=None):
    n = len(units)
    for i in range(n + nstage - 1):
        for s_ in (order if order is not None else (range(nstage - 1, -1, -1) if oldest_first else range(nstage))):
            u = i - s_
            if 0 <= u < n and units[u][s_] is not None:
                units[u][s_]()


def _pk(v, nchunk):
    return np.ascontiguousarray(np.asarray(v, np.float32).reshape(nchunk, 128).T)


class VecLayout:
    def __init__(self):
        self.cols = {}
        self.n = 0

    def add(self, name, width):
        self.cols[name] = (self.n, width)
        self.n += width

    def sl(self, name):
        a, w = self.cols[name]
        return slice(a, a + w)


VL = VecLayout()
for _n, _w in [("ssm_norm", 8), ("kv_norm", 8), ("attn_norm", 8), ("ffn_norm0", 8), ("ffn_norm1", 8),
               ("final_norm", 8), ("ssm_conv_w", 24 * 4), ("ssm_conv_b", 24), ("ssm_d", 16),
               ("gate_norm", 16), ("ffn_conv_w0", 44 * 3), ("ffn_conv_b0", 44), ("ffn_conv_w1", 44 * 3),
               ("ffn_conv_b1", 44), ("dt_bias", 32), ("a_log", 32), ("eps", 1), ("one", 1), ("ssm_d_bc", 32)]:
    VL.add(_n, _w)


def pack_vecs(inp):
    out = np.zeros((128, VL.n), np.float32)
    out[:, VL.sl("ssm_norm")] = _pk(inp["ssm_norm_w"][0], 8)
    out[:, VL.sl("kv_norm")] = _pk(inp["kv_norm_w"], 8)
    out[:, VL.sl("attn_norm")] = _pk(inp["attn_norm_w"][0], 8)
    out[:, VL.sl("ffn_norm0")] = _pk(inp["ffn_norm_w"][0], 8)
    out[:, VL.sl("ffn_norm1")] = _pk(inp["ffn_norm_w"][1], 8)
    out[:, VL.sl("final_norm")] = _pk(inp["final_norm_w"], 8)
    cw = np.asarray(inp["ssm_conv_w"][0], np.float32)
    out[:, VL.sl("ssm_conv_w")] = cw.reshape(4, 24, 128).transpose(2, 1, 0).reshape(128, 96)
    out[:, VL.sl("ssm_conv_b")] = _pk(inp["ssm_conv_b"][0], 24)
    out[:, VL.sl("ssm_d")] = _pk(np.repeat(np.asarray(inp["ssm_d"][0], np.float32), HP), 16)
    out[:, VL.sl("gate_norm")] = _pk(inp["ssm_gate_norm_w"][0], 16)
    for l in range(2):
        fw = np.asarray(inp["ffn_conv_w"][l], np.float32)
        out[:, VL.sl(f"ffn_conv_w{l}")] = fw.reshape(3, 44, 128).transpose(2, 1, 0).reshape(128, 132)
        out[:, VL.sl(f"ffn_conv_b{l}")] = _pk(inp["ffn_conv_b"][l], 44)
    out[:, VL.sl("dt_bias")] = np.broadcast_to(np.asarray(inp["ssm_dt_bias"][0], np.float32), (128, 32))
    out[:, VL.sl("ssm_d_bc")] = np.broadcast_to(np.asarray(inp["ssm_d"][0], np.float32), (128, 32))
    out[:, VL.sl("eps")] = EPS
    out[:, VL.sl("one")] = 1.0
    out[:, VL.sl("a_log")] = np.broadcast_to(np.asarray(inp["ssm_a_log"][0], np.float32), (128, 32))
    return out


def make_consts():
    k = np.arange(128)
    c = np.zeros((128, NCONST, 128), np.float32)
    c[:, 0, :] = np.eye(128)
    c[:, 1, :] = (k[:, None] <= k[None, :])
    c[:, 2, :] = (k[:, None] > k[None, :])
    c[:, 3, :] = 1.0
    c[:, 4, :] = (k[:, None] >= k[None, :])
    c[:, 5, :] = (k[:, None] < k[None, :])
    q = np.arange(TT)
    for j in range(4):
        c[:, NCB + 4 * j:NCB + 4 + 4 * j, :] = ((k[:, None] + 128 * j) < q[None, :]).astype(np.float32).reshape(128, 4, 128)
    return c.reshape(128, NCONST * 128)


class Builder:
    def __init__(self, T, phases, debug=()):
        self.T = T
        self.NT = T // TT
        self.phases = phases
        self.debug = debug

    def dram(self, name, shape, dt, kind="Internal"):
        if kind == "Internal" and name in getattr(self, "extra_out", ()):
            kind = "ExternalOutput"
        return self.nc.dram_tensor(name, list(shape), dt, kind=kind).ap()

    def build(self):
        nc = bass.Bass("TRN2", target_bir_lowering=False)
        self.nc = nc
        T = self.T
        self.xT = self.dram("xT", [D, T], F32, "ExternalInput")
        self.vecs_d = self.dram("vecs", [128, VL.n], F32, "ExternalInput")
        self.consts_d = self.dram("consts", [128, NCONST * 128], F32, "ExternalInput")
        self.w = {}
        for name, shape in [("ssm_in_w", [D, INP]), ("ssm_out_w", [DI, D]), ("w_k", [D, D]), ("w_v", [D, D]),
                            ("w_q", [D, D]), ("w_o", [D, D]), ("ffn_up_w0", [D, 2 * DFF]),
                            ("ffn_up_w1", [D, 2 * DFF]), ("ffn_down_w0", [DFF, D]), ("ffn_down_w1", [DFF, D])]:
            self.w[name] = self.dram(name, shape, F32, "ExternalInput")
        self.outT = self.dram("outT", [D, T], F32, "ExternalOutput")
        self.hA = self.dram("hA", [D, T], F32)
        self.hB = self.dram("hB", [D, T], F32)
        with ExitStack() as stack:
            self.stack = stack
            P = Prog(nc, stack)
            self.P = P
            self.vecs = stack.enter_context(nc.sbuf_tensor("vecs_sb", [128, VL.n], F32))
            self.cf = stack.enter_context(nc.sbuf_tensor("cf_sb", [128, NCB, 128], F32))
            self.cb = stack.enter_context(nc.sbuf_tensor("cb_sb", [128, NCB, 128], BF16))
            self.b_const = Buf("const")
            P.op("sp", lambda e: e.dma_start(out=self.vecs[:], in_=self.vecs_d), writes=[self.b_const], dma=True)
            P.op("sp", lambda e: e.dma_start(out=self.cf[:].rearrange("p a b -> p (a b)"), in_=self.consts_d[:, 0:NCB * 128]),
                 writes=[self.b_const], dma=True)
            P.op("pool", lambda e: e.dma_start(out=self.cb[:].rearrange("p a b -> p (a b)"), in_=self.consts_d[:, 0:NCB * 128]),
                 writes=[self.b_const], dma=True)
            for _ in range(EXTRA_NOPS):
                P.op("pool", lambda e: e.memset(self.cb[:, 3, 0:1], 1.0), writes=[self.b_const])
            self.out_toks = []
            for ph in self.phases:
                P.barrier()
                getattr(self, "phase_" + ph[0])(*ph[1:])
            P.drain_dma()
            P.final_wait("sp", self.out_toks)
            P.emit()
        return nc

    def vcol(self, name, i=0, n=1):
        a, _ = VL.cols[name]
        return self.vecs[:, a + i:a + i + n]

    def load_weight_bf16(self, wt, wb, dram_ap, kchunks, ncols, order=None):
        src = dram_ap.rearrange("(k p) n -> p k n", p=128)
        CW = 1024
        blocks = list(range(0, ncols, CW))
        if order is not None:
            blocks = [b for b in order] + [b for b in blocks if b not in order]
        for c0 in blocks:
            c1 = min(ncols, c0 + CW)
            for k in range(kchunks):
                self.P.op("pool", lambda e, k=k, c0=c0, c1=c1: e.dma_start(out=wt[:, k, c0:c1], in_=src[:, k, c0:c1]),
                          writes=[wb[(k, c0)]], dma=True)

    def rms_stats(self, h_sb, b_h, sq, b_sq, ps_stat, b_stat, rstd, b_rstd, nchunk, denom):
        P = self.P
        P.op("act", lambda e: e.activation(out=sq, in_=h_sb, func=AF.Square, scale=1.0 / math.sqrt(denom)),
             reads=[b_h], writes=b_sq)
        for k in range(nchunk):
            P.op("pe", lambda e, k=k: e.matmul(ps_stat[:], lhsT=self.cb[:, 3, :], rhs=sq[:, k, :],
                                               start=(k == 0), stop=(k == nchunk - 1)),
                 reads=b_sq + [self.b_const], writes=[b_stat], inc=(k == nchunk - 1))
        P.op("act", lambda e: e.activation(out=rstd[:], in_=ps_stat[:], func=AF.Ln, bias=self.vcol("eps")),
             reads=[b_stat, self.b_const], writes=[b_rstd])
        P.op("act", lambda e: e.activation(out=rstd[:], in_=rstd[:], func=AF.Exp, scale=-0.5),
             reads=[b_rstd], writes=[b_rstd])

    def dbuf(self, name):
        if not hasattr(self, "_dbufs"):
            self._dbufs = {}
        return self._dbufs.setdefault(name, Buf("dram_" + name))

    def tile_ap(self, name, tt):
        return getattr(self, name)[:, tt * TT:(tt + 1) * TT].rearrange("(k p) t -> p k t", p=128)

    def phase_ffn(self, layer, h_in, h_out):
        nc, P, NT = self.nc, self.P, self.NT
        bd_in, bd_out = self.dbuf(h_in), self.dbuf(h_out)
        NP = DFF // 128
        XW = TT + 2
        with ExitStack() as st:
            sb = lambda n, s, d: st.enter_context(nc.sbuf_tensor(f"f{layer}_{n}", s, d))
            ps = lambda n: st.enter_context(nc.psum_tensor(f"f{layer}_{n}", [128, TT], F32))
            wup = sb("wup", [128, 8, 2 * DFF], BF16)
            wdn = sb("wdn", [128, NP, D], BF16)
            b_wup, b_wdn = Buf("wup"), Buf("wdn")
            order = []
            for j in range(NP):
                for col in (j * 128, DFF + j * 128):
                    if col // 1024 * 1024 not in order:
                        order.append(col // 1024 * 1024)
            self.load_weight_bf16(wup, b_wup, self.w[f"ffn_up_w{layer}"], 8, 2 * DFF, order)
            self.load_weight_bf16(wdn, b_wdn, self.w[f"ffn_down_w{layer}"], NP, D)
            h_sb = sb("h", [128, 8, TT], F32); b_h = Buf("h")
            rstd = sb("rstd", [128, TT], F32); b_rstd = Buf("rstd")
            u = sb("u", [128, 8, TT], BF16); b_u = Buf("u")
            g = sb("g", [128, NP, TT], BF16); b_g = Buf("g")
            scr = sb("scr", [128, 3 * XW + 3 * TT], F32)
            xpad = [scr[:, i * XW:(i + 1) * XW] for i in range(3)]
            acc = [scr[:, 3 * XW + i * TT:3 * XW + (i + 1) * TT] for i in range(3)]
            b_xpad = [Buf(f"xpad{i}") for i in range(3)]
            b_acc = [Buf(f"acc{i}") for i in range(3)]
            sq = scr[:, 0:4 * TT].bitcast(BF16).rearrange("p (k t) -> p k t", k=8)
            b_sq = b_xpad + b_acc
            hres = [sb(f"hres{i}", [128, TT], F32) for i in range(2)]; b_hres = [Buf(f"hres{i}") for i in range(2)]
            hout = [sb(f"hout{i}", [128, TT], F32) for i in range(2)]; b_hout = [Buf(f"hout{i}") for i in range(2)]
            carry = sb("carry", [128, 2 * NP, 2], F32); b_carry = Buf("carry")
            ps_stat = ps("stat"); b_stat = Buf("stat", True)
            ps_x = [ps(f"x{i}") for i in range(4)]; b_psx = [Buf(f"psx{i}", True) for i in range(4)]
            ps_o = [ps(f"o{i}") for i in range(2)]; b_pso = [Buf(f"pso{i}", True) for i in range(2)]
            nw = f"ffn_norm{layer}"
            cwn, cbn = f"ffn_conv_w{layer}", f"ffn_conv_b{layer}"
            P.op("pool", lambda e: e.memset(carry[:], 0.0), writes=[b_carry])

            def load_h(tt):
                P.op("sp", lambda e: e.dma_start(out=h_sb[:], in_=self.tile_ap(h_in, tt)),
                     reads=Prog.rows(bd_in, tt), writes=[b_h], dma=True)

            def prologue(tt):
                self.rms_stats(h_sb[:], b_h, sq, b_sq, ps_stat, b_stat, rstd, b_rstd, 8, float(D))
                for k in range(8):
                    P.op("dve", lambda e, k=k: e.scalar_tensor_tensor(
                        out=u[:, k, :], in0=h_sb[:, k, :], scalar=self.vcol(nw, k), in1=rstd[:],
                        op0=ALU.mult, op1=ALU.mult), reads=[b_h, b_rstd, self.b_const], writes=[b_u[k]])
                if tt + 1 < NT:
                    load_h(tt + 1)

            def down(tt, ms):
                for m in ms:
                    o = m % 2
                    for kk in range(NP):
                        P.op("pe", lambda e, kk=kk, m=m, o=o: e.matmul(
                            ps_o[o][:], lhsT=wdn[:, kk, m * 128:(m + 1) * 128], rhs=g[:, kk, :],
                            start=(kk == 0), stop=(kk == NP - 1)),
                            reads=[b_g[kk], b_wdn[(kk, 0)]], writes=[b_pso[o]], inc=(kk == NP - 1))
                    P.op("sp", lambda e, m=m, o=o: e.dma_start(
                        out=hres[o][:], in_=getattr(self, h_in)[m * 128:(m + 1) * 128, tt * TT:(tt + 1) * TT]),
                        reads=[bd_in[(tt, m)]], writes=[b_hres[o]], dma=True)
                    P.op("dve", lambda e, o=o: e.tensor_tensor(
                        out=hout[o][:], in0=hres[o][:], in1=ps_o[o][:], op=ALU.add),
                        reads=[b_hres[o], b_pso[o]], writes=[b_hout[o]])
                    P.op("pool", lambda e, m=m, o=o: e.dma_start(
                        out=getattr(self, h_out)[m * 128:(m + 1) * 128, tt * TT:(tt + 1) * TT], in_=hout[o][:]),
                        reads=[b_hout[o]], writes=[bd_out[(tt, m)]], dma=True)

            load_h(0)
            prologue(0)
            xi = 0
            for tt in range(NT):
                if DEBUG_LEVEL < 2:
                    break
                for j in range(NP if DEBUG_LEVEL >= 4 else 1):
                    res = []
                    for half in range(2):
                        c = j + half * NP
                        s = xi % 4
                        a = xi % 3
                        xi += 1
                        for k in range(8):
                            P.op("pe", lambda e, k=k, c=c, s=s: e.matmul(
                                ps_x[s][:], lhsT=wup[:, k, c * 128:(c + 1) * 128], rhs=u[:, k, :],
                                start=(k == 0), stop=(k == 7)),
                                reads=[b_u[k], b_wup[(k, (c * 128) // 1024 * 1024)]], writes=[b_psx[s]], inc=(k == 7))
                        P.op("pool", lambda e, c=c, a=a: e.tensor_copy(out=xpad[a][:, 0:2], in_=carry[:, c, :]),
                             reads=[b_carry[c]], writes=[b_xpad[a]["halo"]])
                        P.op("act", lambda e, s=s, a=a: e.copy(out=xpad[a][:, 2:XW], in_=ps_x[s][:]),
                             reads=[b_psx[s]], writes=[b_xpad[a]["body"]])
                        P.op("act", lambda e, c=c, s=s, a=a: e.activation(
                            out=acc[a], in_=ps_x[s][:], func=AF.Identity, scale=self.vcol(cwn, c * 3 + 2),
                            bias=self.vcol(cbn, c)),
                            reads=[b_psx[s], self.b_const], writes=[b_acc[a]])
                        for tap in (1, 0):
                            P.op("dve", lambda e, c=c, a=a, tap=tap: e.scalar_tensor_tensor(
                                out=acc[a], in0=xpad[a][:, tap:tap + TT], scalar=self.vcol(cwn, c * 3 + tap),
                                in1=acc[a], op0=ALU.mult, op1=ALU.add),
                                reads=[b_xpad[a], b_acc[a], self.b_const], writes=[b_acc[a]])
                        if DEBUG_LEVEL >= 3:
                            P.op("pool", lambda e, c=c, a=a: e.tensor_copy(out=carry[:, c, :], in_=xpad[a][:, TT:XW]),
                                 reads=[b_xpad[a]], writes=[b_carry[c]])
                        res.append(a)
                    P.op("act", lambda e, a=res[0]: e.activation(out=acc[a], in_=acc[a], func=AF.Silu),
                         reads=[b_acc[res[0]]], writes=[b_acc[res[0]]])
                    P.op("dve", lambda e, j=j, a0=res[0], a1=res[1]: e.tensor_tensor(
                        out=g[:, j, :], in0=acc[a0], in1=acc[a1], op=ALU.mult),
                        reads=[b_acc[res[0]], b_acc[res[1]]], writes=[b_g[j]])
                if DEBUG_LEVEL < 5:
                    break
                down(tt, range(0, 4))
                if tt + 1 < NT:
                    prologue(tt + 1)
                down(tt, range(4, 8))

    def phase_m_in(self, h_in):
        nc, P, NT, T = self.nc, self.P, self.NT, self.T
        bd_in = self.dbuf(h_in)
        for n, shp, dt in [("zsT", [DI, T], F32), ("xs_tm", [T, DI], BF16),
                           ("BT", [NG * NS, T], BF16), ("CT", [NG * NS, T], BF16),
                           ("B_tm", [T, NG * NS], BF16), ("dt_tm", [T, NH], F32)]:
            setattr(self, n, self.dram(n, shp, dt))
        bd = {n: self.dbuf(n) for n in ("zsT", "xs_tm", "BT", "CT", "B_tm", "dt_tm")}
        XW = TT + 3
        NB = TT // 128
        NXP, NAC, NXC, NZO = 3, 5, 3, 3
        with ExitStack() as st:
            sb = lambda n, s, d: st.enter_context(nc.sbuf_tensor(f"mi_{n}", s, d))
            win = sb("win", [128, 8, INP], BF16); b_win = Buf("win")
            self.load_weight_bf16(win, b_win, self.w["ssm_in_w"], 8, INP)
            h_sb = sb("h", [128, 8, TT], F32); b_h = Buf("h")
            rstd = sb("rstd", [128, TT], F32); b_rstd = Buf("rstd")
            u2 = [sb(f"u{i}", [128, 8, TT], BF16) for i in range(2)]; b_u2 = [Buf(f"u{i}") for i in range(2)]
            sq = sb("sq", [128, 8, TT], BF16); b_sq = [Buf("sq")]
            xpad = [sb(f"xpad{i}", [128, XW], F32) for i in range(NXP)]; b_xpad = [Buf(f"xpad{i}") for i in range(NXP)]
            acc = [sb(f"acc{i}", [128, TT], F32) for i in range(NAC)]; b_acc = [Buf(f"acc{i}") for i in range(NAC)]
            zo = [sb(f"zo{i}", [128, TT], F32) for i in range(NZO)]; b_zo = [Buf(f"zo{i}") for i in range(NZO)]
            xcb = [sb(f"xcb{i}", [128, TT], BF16) for i in range(NXC)]; b_xcb = [Buf(f"xcb{i}") for i in range(NXC)]
            xtm = sb("xtm", [128, NB, DI], BF16); b_xtm = Buf("xtm")
            btm = sb("btm", [128, NB, NG * NS], BF16); b_btm = Buf("btm")
            dts = sb("dts", [128, NB, NH], F32); b_dts = Buf("dts")
            carry = sb("carry", [128, 24, 3], F32); b_carry = Buf("carry")
            ps_stat = st.enter_context(nc.psum_tensor("mi_stat", [128, TT], F32)); b_stat = Buf("stat", True)
            ps_x = [st.enter_context(nc.psum_tensor(f"mi_x{i}", [128, TT], F32)) for i in range(4)]
            b_psx = [Buf(f"psx{i}", True) for i in range(4)]
            ps_tr = [st.enter_context(nc.psum_tensor(f"mi_tr{i}", [128, 2 * NB, 128], BF16)) for i in range(2)]
            b_pstr = [Buf(f"pstr{i}", True) for i in range(2)]
            P.op("pool", lambda e: e.memset(carry[:], 0.0), writes=[b_carry])

            def load_h(tt):
                P.op("sp", lambda e: e.dma_start(out=h_sb[:], in_=self.tile_ap(h_in, tt)),
                     reads=Prog.rows(bd_in, tt), writes=[b_h], dma=True)

            def prologue(tt):
                u, b_u = u2[tt % 2], b_u2[tt % 2]
                self.rms_stats(h_sb[:], b_h, sq[:], b_sq, ps_stat, b_stat, rstd, b_rstd, 8, float(D))
                for k in range(8):
                    P.op("dve", lambda e, k=k: e.scalar_tensor_tensor(
                        out=u[:, k, :], in0=h_sb[:, k, :], scalar=self.vcol("ssm_norm", k), in1=rstd[:],
                        op0=ALU.mult, op1=ALU.mult), reads=[b_h, b_rstd, self.b_const], writes=[b_u[k]])
                if tt + 1 < NT:
                    load_h(tt + 1)

            def wkey(col):
                return (col // 1024) * 1024

            def mm(col, s_, tt):
                u, b_u = u2[tt % 2], b_u2[tt % 2]
                for k in range(8):
                    P.op("pe", lambda e, k=k: e.matmul(
                        ps_x[s_][:], lhsT=win[:, k, col:col + 128], rhs=u[:, k, :], start=(k == 0), stop=(k == 7)),
                        reads=[b_u[k], b_win[(k, wkey(col))]], writes=[b_psx[s_]], inc=(k == 7))

            load_h(0)
            prologue(0)
            cnt = {"x": 0, "a": 0, "p": 0, "c": 0, "z": 0, "t": 0}

            def nxt(k, m):
                v = cnt[k] % m
                cnt[k] += 1
                return v

            for tt in range(NT):
                tsl = slice(tt * TT, (tt + 1) * TT)
                units = []
                for c in range(16):
                    s_, o = nxt("x", 4), nxt("z", NZO)

                    def z1(c=c, s_=s_, tt=tt):
                        mm(c * 128, s_, tt)

                    def z2(c=c, s_=s_, o=o, tsl=tsl, tt=tt):
                        P.op("act", lambda e: e.activation(out=zo[o][:], in_=ps_x[s_][:], func=AF.Silu),
                             reads=[b_psx[s_]], writes=[b_zo[o]])
                        P.op("pool", lambda e: e.dma_start(out=self.zsT[c * 128:(c + 1) * 128, tsl], in_=zo[o][:]),
                             reads=[b_zo[o]], writes=[bd["zsT"][(tt, c)]], dma=True)

                    units.append([z1, z2, None, None])
                if tt + 1 < NT:
                    units.append([(lambda tt=tt: prologue(tt + 1)), None, None, None])
                for c in range(24):
                    s_, a, xp, o = nxt("x", 4), nxt("a", NAC), nxt("p", NXP), nxt("c", NXC)
                    tr = nxt("t", 2) if c < 20 else 0

                    def x1(c=c, s_=s_, a=a, xp=xp, tt=tt):
                        mm(DI + c * 128, s_, tt)
                        P.op("pool", lambda e: e.tensor_copy(out=xpad[xp][:, 0:3], in_=carry[:, c, :]),
                             reads=[b_carry[c]], writes=[b_xpad[xp]["halo"]])
                        P.op("act", lambda e: e.copy(out=xpad[xp][:, 3:XW], in_=ps_x[s_][:]),
                             reads=[b_psx[s_]], writes=[b_xpad[xp]["body"]])
                        P.op("act", lambda e: e.activation(
                            out=acc[a][:], in_=ps_x[s_][:], func=AF.Identity, scale=self.vcol("ssm_conv_w", c * 4 + 3),
                            bias=self.vcol("ssm_conv_b", c)),
                            reads=[b_psx[s_], self.b_const], writes=[b_acc[a]])

                    def x2(c=c, a=a, xp=xp):
                        for tap in (2, 1, 0):
                            P.op("dve", lambda e, tap=tap: e.scalar_tensor_tensor(
                                out=acc[a][:], in0=xpad[xp][:, tap:tap + TT], scalar=self.vcol("ssm_conv_w", c * 4 + tap),
                                in1=acc[a][:], op0=ALU.mult, op1=ALU.add),
                                reads=[b_xpad[xp], b_acc[a], self.b_const], writes=[b_acc[a]])
                        P.op("pool", lambda e: e.tensor_copy(out=carry[:, c, :], in_=xpad[xp][:, TT:XW]),
                             reads=[b_xpad[xp]], writes=[b_carry[c]])

                    def x3(c=c, a=a, o=o, tsl=tsl, tt=tt):
                        P.op("act", lambda e: e.activation(out=acc[a][:], in_=acc[a][:], func=AF.Silu),
                             reads=[b_acc[a]], writes=[b_acc[a]])
                        P.op("dve", lambda e: e.tensor_copy(out=xcb[o][:], in_=acc[a][:]),
                             reads=[b_acc[a]], writes=[b_xcb[o]])
                        if c >= 16:
                            dst, nm = (self.BT, "BT") if c < 20 else (self.CT, "CT")
                            cc = (c - 16) % 4
                            P.op("pool", lambda e: e.dma_start(out=dst[cc * 128:(cc + 1) * 128, tsl], in_=xcb[o][:]),
                                 reads=[b_xcb[o]], writes=[bd[nm][(tt, cc)]], dma=True)

                    def x4(c=c, o=o, tr=tr):
                        for blk in range(NB):
                            P.op("pe", lambda e, blk=blk: e.transpose(
                                out=ps_tr[tr][:, blk, :], in_=xcb[o][:, blk * 128:(blk + 1) * 128], identity=self.cb[:, 0, :]),
                                reads=[b_xcb[o], self.b_const], writes=[b_pstr[tr]], inc=(blk == NB - 1))
                        if c < 16:
                            P.op("dve", lambda e: e.tensor_copy(out=xtm[:, :, c * 128:(c + 1) * 128], in_=ps_tr[tr][:, 0:NB, :]),
                                 reads=[b_pstr[tr]], writes=[b_xtm[c]])
                        else:
                            cc = c - 16
                            P.op("dve", lambda e: e.tensor_copy(out=btm[:, :, cc * 128:(cc + 1) * 128], in_=ps_tr[tr][:, 0:NB, :]),
                                 reads=[b_pstr[tr]], writes=[b_btm[cc]])

                    units.append([x1, x2, x3, x4 if c < 20 else None])
                s_ = nxt("x", 4)
                dtv = dts[:].rearrange("p b h -> p (b h)")

                def d1(s_=s_, tt=tt):
                    u, b_u = u2[tt % 2], b_u2[tt % 2]
                    for blk in range(NB):
                        for k in range(8):
                            P.op("pe", lambda e, k=k, blk=blk: e.matmul(
                                ps_x[s_][:, blk * NH:(blk + 1) * NH], lhsT=u[:, k, blk * 128:(blk + 1) * 128],
                                rhs=win[:, k, DI + CONVD:INP], start=(k == 0), stop=(k == 7)),
                                reads=[b_u[k], b_win[(k, wkey(DI + CONVD))]], writes=[b_psx[s_]], inc=(k == 7 and blk == NB - 1))

                def d2(s_=s_):
                    P.op("dve", lambda e: e.tensor_tensor(
                        out=dts[:], in0=ps_x[s_][:, 0:NB * NH].rearrange("p (b h) -> p b h", b=NB),
                        in1=self.vecs[:, VL.sl("dt_bias")].unsqueeze(1).to_broadcast([128, NB, NH]), op=ALU.add),
                        reads=[b_psx[s_], self.b_const], writes=[b_dts])

                def d3():
                    P.op("act", lambda e: e.activation(out=dtv, in_=dtv, func=AF.Exp), reads=[b_dts], writes=[b_dts])

                def d4(tt=tt):
                    P.op("act", lambda e: e.activation(out=dtv, in_=dtv, func=AF.Ln, bias=self.vcol("one")),
                         reads=[b_dts, self.b_const], writes=[b_dts])

                units.append([d1, d2, d3, d4])
                run_pipeline(units, 4)
                tsl_tm = lambda ap, tt=tt: ap[tt * TT:(tt + 1) * TT, :].rearrange("(b p) f -> p b f", p=128)
                P.op("pool", lambda e, f=tsl_tm: e.dma_start(out=f(self.dt_tm), in_=dts[:]),
                     reads=[b_dts], writes=[bd["dt_tm"][tt]], dma=True)
                P.op("pool", lambda e, f=tsl_tm: e.dma_start(out=f(self.xs_tm), in_=xtm[:]),
                     reads=[b_xtm], writes=[bd["xs_tm"][tt]], dma=True)
                P.op("pool", lambda e, f=tsl_tm: e.dma_start(out=f(self.B_tm), in_=btm[:]),
                     reads=[b_btm], writes=[bd["B_tm"][tt]], dma=True)

    def phase_ssd(self):
        nc, P, T = self.nc, self.P, self.T
        NC = T // 128
        self.yT = self.dram("yT", [DI, T], F32)
        bd = {n: self.dbuf(n) for n in ("xs_tm", "BT", "CT", "B_tm", "dt_tm", "yT")}
        with ExitStack() as st:
            sb = lambda n, s, d: st.enter_context(nc.sbuf_tensor(f"sd_{n}", s, d))
            dbl = lambda n, s, d: [sb(f"{n}{i}", s, d) for i in range(2)]
            bufs = lambda n: [Buf(f"{n}{i}") for i in range(2)]
            tri = lambda n, s, d: [sb(f"{n}{i}", s, d) for i in range(4)]
            tbufs = lambda n: [Buf(f"{n}{i}") for i in range(4)]
            xs_c = tri("xs", [128, NH, HP], BF16); b_xs = tbufs("xs")
            bt_c = tri("bt", [128, NG * NS], BF16); b_bt = tbufs("bt")
            BTc = tri("BT", [128, NG, 128], BF16); b_BT = tbufs("BT")
            CTc = tri("CT", [128, NG, 128], BF16); b_CT = tbufs("CT")
            dt_c = tri("dt", [128, NH], F32); b_dt = tbufs("dt")
            a_sb = sb("a", [128, NH], F32); b_a = Buf("a")
            ad = dbl("ad", [128, NH], F32); b_ad = bufs("ad")
            xd = dbl("xd", [128, NH, HP], BF16); b_xd = bufs("xd")
            xdd = dbl("xdd", [128, NH, HP], BF16); b_xdd = bufs("xdd")
            ex2 = dbl("ex2", [128, 2 * NH], F32); b_ex2 = bufs("ex2")
            seg = dbl("seg", [128, NH, 128], BF16); b_seg = bufs("seg")
            CBm = dbl("CBm", [128, 128], F32); b_CBm = bufs("CBm")
            E = dbl("E", [128, 8, 128], F32); b_E = bufs("E")
            EA = dbl("EA", [128, 8, 128], F32); b_EA = bufs("EA")
            MT = dbl("MT", [128, 8, 128], BF16); b_MT = bufs("MT")
            CsT = dbl("CsT", [128, 8, 128], BF16); b_CsT = bufs("CsT")
            S = sb("S", [128, NH, HP], F32); b_S = Buf("S")
            prev = dbl("prev", [128, NH, HP], BF16); b_prev = bufs("prev")
            yo = dbl("yo", [128, 16, 128], F32); b_yo = bufs("yo")
            pst = lambda n, s, d=F32: st.enter_context(nc.psum_tensor(f"sd_{n}", s, d))
            Dps = pst("D", [128, 2, 512]); b_D = Buf("D", True)
            Aps = pst("A", [128, 2, 512]); b_A = Buf("A", True)
            misc = pst("misc", [128, 512]); b_misc = Buf("misc", True)
            yps = [pst(f"y{i}", [128, 4, 128]) for i in range(2)]; b_yps = [Buf(f"yps{i}", True) for i in range(2)]
            stps = pst("st", [128, 8, HP]); b_st = Buf("st", True)
            TRI, SUP, ONE = self.cf[:, 1, :], self.cf[:, 2, :], self.cf[:, 3, :]
            P.op("act", lambda e: e.activation(out=a_sb[:], in_=self.vecs[:, VL.sl("a_log")], func=AF.Exp),
                 reads=[self.b_const], writes=[b_a])
            P.op("dve", lambda e: e.tensor_scalar(out=a_sb[:], in0=a_sb[:], scalar1=-1.0, scalar2=None, op0=ALU.mult),
                 reads=[b_a], writes=[b_a])
            dh = sb("dh", [128, NH], BF16); dl = sb("dl", [128, NH], BF16); dr = sb("dr", [128, NH], F32); b_dv = Buf("dv")
            Dhi = sb("Dhi", [128, NH, 128], BF16); Dlo = sb("Dlo", [128, NH, 128], BF16); b_Dm = Buf("Dm")
            dbc = self.vecs[:, VL.sl("ssm_d_bc")]
            P.op("dve", lambda e: e.tensor_copy(out=dh[:], in_=dbc), reads=[self.b_const], writes=[b_dv["h"]])
            P.op("dve", lambda e: e.tensor_tensor(out=dr[:], in0=dbc, in1=dh[:], op=ALU.subtract),
                 reads=[self.b_const, b_dv["h"]], writes=[b_dv["r"]])
            P.op("dve", lambda e: e.tensor_copy(out=dl[:], in_=dr[:]), reads=[b_dv["r"]], writes=[b_dv["l"]])
            for dst, src, key in ((Dhi, dh, "hi"), (Dlo, dl, "lo")):
                P.op("dve", lambda e, dst=dst, src=src: e.tensor_tensor(
                    out=dst[:], in0=self.cf[:, 0, :].unsqueeze(1).to_broadcast([128, NH, 128]),
                    in1=src[:].unsqueeze(2).to_broadcast([128, NH, 128]), op=ALU.mult),
                    reads=[b_dv, self.b_const], writes=[b_Dm[key]])
            P.op("pool", lambda e: e.memset(S[:], 0.0), writes=[b_S])
            P.op("pool", lambda e: e.memset(prev[0][:], 0.0), writes=[b_prev[0]])

            def load(c):
                l3 = c % 4
                tok = slice(c * 128, (c + 1) * 128)
                tt = c // (TT // 128)
                P.op("sp", lambda e: e.dma_start(out=xs_c[l3][:].rearrange("p h q -> p (h q)"), in_=self.xs_tm[tok, :]),
                     reads=[bd["xs_tm"][tt]], writes=[b_xs[l3]], dma=True)
                P.op("sp", lambda e: e.dma_start(out=bt_c[l3][:], in_=self.B_tm[tok, :]),
                     reads=[bd["B_tm"][tt]], writes=[b_bt[l3]], dma=True)
                P.op("sp", lambda e: e.dma_start(out=BTc[l3][:], in_=self.BT.rearrange("(g n) t -> n g t", n=128)[:, :, tok]),
                     reads=Prog.rows(bd["BT"], tt, 4), writes=[b_BT[l3]], dma=True)
                P.op("sp", lambda e: e.dma_start(out=CTc[l3][:], in_=self.CT.rearrange("(g n) t -> n g t", n=128)[:, :, tok]),
                     reads=Prog.rows(bd["CT"], tt, 4), writes=[b_CT[l3]], dma=True)
                P.op("sp", lambda e: e.dma_start(out=dt_c[l3][:], in_=self.dt_tm[tok, :]),
                     reads=[bd["dt_tm"][tt]], writes=[b_dt[l3]], dma=True)

            adh = dbl("adh", [128, NH], BF16); adl = dbl("adl", [128, NH], BF16); adr = dbl("adr", [128, NH], F32)
            segl = dbl("segl", [128, NH, 128], BF16); b_segl = bufs("segl")
            b_sp_ = bufs("adsplit")
            TRIb, SUPb, ONEb = self.cb[:, 1, :], self.cb[:, 2, :], self.cb[:, 3, :]
            load(0)
            if NC > 1:
                load(1)
            units = []
            pres = []
            presB = []
            gi = 0
            for c in range(NC):
                i = c % 2
                cur, nxt = c % 2, (c + 1) % 2

                def pre(c=c, i=i, l3=c % 4):
                    if c + 2 < NC:
                        load(c + 2)
                    P.op("dve", lambda e: e.tensor_tensor(out=ad[i][:], in0=dt_c[l3][:], in1=a_sb[:], op=ALU.mult),
                         reads=[b_dt[l3], b_a], writes=[b_ad[i]])
                    P.op("dve", lambda e: e.tensor_copy(out=adh[i][:], in_=ad[i][:]), reads=[b_ad[i]], writes=[b_sp_[i]["h"]])
                    P.op("dve", lambda e: e.tensor_tensor(out=adr[i][:], in0=ad[i][:], in1=adh[i][:], op=ALU.subtract),
                         reads=[b_ad[i], b_sp_[i]["h"]], writes=[b_sp_[i]["r"]])
                    P.op("dve", lambda e: e.tensor_copy(out=adl[i][:], in_=adr[i][:]), reads=[b_sp_[i]["r"]], writes=[b_sp_[i]["l"]])
                    P.op("dve", lambda e: e.tensor_tensor(
                        out=seg[i][:], in0=TRIb.unsqueeze(1).to_broadcast([128, NH, 128]),
                        in1=adh[i][:].unsqueeze(2).to_broadcast([128, NH, 128]), op=ALU.mult),
                        reads=[b_sp_[i]["h"], self.b_const], writes=[b_seg[i]])
                    P.op("dve", lambda e: e.tensor_tensor(
                        out=segl[i][:], in0=TRIb.unsqueeze(1).to_broadcast([128, NH, 128]),
                        in1=adl[i][:].unsqueeze(2).to_broadcast([128, NH, 128]), op=ALU.mult),
                        reads=[b_sp_[i]["l"], self.b_const], writes=[b_segl[i]])
                    P.op("dve", lambda e: e.tensor_tensor(
                        out=xd[i][:], in0=xs_c[l3][:], in1=dt_c[l3][:].unsqueeze(2).to_broadcast([128, NH, HP]), op=ALU.mult),
                        reads=[b_xs[l3], b_dt[l3]], writes=[b_xd[i]])

                def preB(c=c, i=i, l3=c % 4):
                    P.op("pe", lambda e: e.matmul(misc[:, 256:256 + NH], lhsT=SUP, rhs=ad[i][:], start=True, stop=True),
                         reads=[b_ad[i], self.b_const], writes=[b_misc], inc=False)
                    P.op("pe", lambda e: e.matmul(misc[:, 256 + NH:256 + 2 * NH], lhsT=ONE, rhs=ad[i][:], start=True, stop=True),
                         reads=[b_ad[i], self.b_const], writes=[b_misc])
                    P.op("act", lambda e: e.activation(out=ex2[i][:], in_=misc[:, 256:256 + 2 * NH], func=AF.Exp),
                         reads=[b_misc], writes=[b_ex2[i]])
                    P.op("dve", lambda e: e.tensor_tensor(out=adr[i][:], in0=dt_c[l3][:], in1=ex2[i][:, 0:NH], op=ALU.mult),
                         reads=[b_dt[l3], b_ex2[i], b_sp_[i]["l"]], writes=[b_sp_[i]["r"]])
                    P.op("dve", lambda e: e.tensor_tensor(
                        out=xdd[i][:], in0=xs_c[l3][:], in1=adr[i][:].unsqueeze(2).to_broadcast([128, NH, HP]), op=ALU.mult),
                        reads=[b_xs[l3], b_sp_[i]["r"]], writes=[b_xdd[i]])


                presB.append(preB)
                pres.append(pre)
                for g in range(NG):
                    j = gi % 2
                    gi += 1

                    def u1(c=c, i=i, g=g, j=j, l3=c % 4):
                        if c == 0 and g == 0:
                            pres[0]()
                            presB[0]()
                        if g == 1 and c + 1 < NC:
                            pres[c + 1]()
                        if g == 3 and c + 1 < NC:
                            presB[c + 1]()
                        P.op("pe", lambda e: e.matmul(misc[:, j * 128:(j + 1) * 128], lhsT=BTc[l3][:, g, :],
                                                      rhs=CTc[l3][:, g, :], start=True, stop=True),
                             reads=[b_BT[l3], b_CT[l3]], writes=[b_misc])
                        P.op("dve", lambda e: e.tensor_tensor(out=CBm[j][:], in0=misc[:, j * 128:(j + 1) * 128], in1=TRI, op=ALU.mult),
                             reads=[b_misc, self.b_const], writes=[b_CBm[j]])
                        for ps_, b_ps, lhs in ((Dps, b_D, SUPb), (Aps, b_A, ONEb)):
                            for half in range(2):
                                hs = slice(g * 8 + half * 4, g * 8 + half * 4 + 4)
                                P.op("pe", lambda e, ps_=ps_, lhs=lhs, half=half, hs=hs: e.matmul(
                                    ps_[:, half, :], lhsT=lhs, rhs=seg[i][:, hs, :].rearrange("p h l -> p (h l)"),
                                    start=True, stop=False),
                                    reads=[b_seg[i], self.b_const], writes=[b_ps], inc=False)
                                P.op("pe", lambda e, ps_=ps_, lhs=lhs, half=half, hs=hs: e.matmul(
                                    ps_[:, half, :], lhsT=lhs, rhs=segl[i][:, hs, :].rearrange("p h l -> p (h l)"),
                                    start=False, stop=True),
                                    reads=[b_segl[i], self.b_const], writes=[b_ps], inc=(half == 1))

                    def u2(c=c, i=i, g=g, j=j, l3=c % 4):
                        P.op("act", lambda e: e.activation(out=E[j][:].rearrange("p h l -> p (h l)"),
                                                           in_=Dps[:].rearrange("p a b -> p (a b)"), func=AF.Exp),
                             reads=[b_D], writes=[b_E[j]])
                        P.op("act", lambda e: e.activation(out=EA[j][:].rearrange("p h l -> p (h l)"),
                                                           in_=Aps[:].rearrange("p a b -> p (a b)"), func=AF.Exp),
                             reads=[b_A], writes=[b_EA[j]])
                        P.op("dve", lambda e: e.tensor_tensor(
                            out=MT[j][:], in0=E[j][:], in1=CBm[j][:].unsqueeze(1).to_broadcast([128, 8, 128]), op=ALU.mult),
                            reads=[b_E[j], b_CBm[j]], writes=[b_MT[j]])
                        P.op("dve", lambda e: e.tensor_tensor(
                            out=CsT[j][:], in0=EA[j][:], in1=CTc[l3][:, g, :].unsqueeze(1).to_broadcast([128, 8, 128]), op=ALU.mult),
                            reads=[b_EA[j], b_CT[l3]], writes=[b_CsT[j]])

                    def u3(c=c, i=i, g=g, j=j, cur=cur, nxt=nxt, last=(g == NG - 1), l3=c % 4):
                        for hh in range(8):
                            h = g * 8 + hh
                            slot, half = hh // 2, hh % 2
                            tp = (0, 64) if half else None
                            P.op("pe", lambda e, h=h, hh=hh, slot=slot, half=half, tp=tp: e.matmul(
                                yps[j][half * 64:(half + 1) * 64, slot, :], lhsT=xd[i][:, h, :], rhs=MT[j][:, hh, :],
                                start=True, stop=False, tile_position=tp),
                                reads=[b_xd[i], b_MT[j]], writes=[b_yps[j]], inc=False)
                            P.op("pe", lambda e, h=h, hh=hh, slot=slot, half=half, tp=tp: e.matmul(
                                yps[j][half * 64:(half + 1) * 64, slot, :], lhsT=prev[cur][:, h, :], rhs=CsT[j][:, hh, :],
                                start=False, stop=False, tile_position=tp),
                                reads=[b_prev[cur][g], b_CsT[j]], writes=[b_yps[j]], inc=False)
                            for Dm, lastd in ((Dhi, False), (Dlo, True)):
                                P.op("pe", lambda e, h=h, slot=slot, half=half, tp=tp, Dm=Dm, lastd=lastd: e.matmul(
                                    yps[j][half * 64:(half + 1) * 64, slot, :], lhsT=xs_c[l3][:, h, :], rhs=Dm[:, h, :],
                                    start=False, stop=lastd, tile_position=tp),
                                    reads=[b_xs[l3], b_Dm], writes=[b_yps[j]], inc=(lastd and hh == 7))
                        P.op("act", lambda e: e.copy(out=yo[i][:, g * 4:(g + 1) * 4, :], in_=yps[j][:]),
                             reads=[b_yps[j]], writes=[b_yo[i][g]])
                        P.op("pe", lambda e: e.matmul(
                            stps[:].rearrange("p h q -> p (h q)"), lhsT=bt_c[l3][:, g * 128:(g + 1) * 128],
                            rhs=xdd[i][:, g * 8:(g + 1) * 8, :].rearrange("p h q -> p (h q)"), start=True, stop=True),
                            reads=[b_bt[l3], b_xdd[i]], writes=[b_st])
                        P.op("dve", lambda e: e.tensor_tensor(
                            out=S[:, g * 8:(g + 1) * 8, :], in0=S[:, g * 8:(g + 1) * 8, :],
                            in1=ex2[i][:, NH + g * 8:NH + (g + 1) * 8].unsqueeze(2).to_broadcast([128, 8, HP]), op=ALU.mult),
                            reads=[b_S[g], b_ex2[i]], writes=[b_S[g]])
                        P.op("dve", lambda e: e.tensor_tensor(
                            out=S[:, g * 8:(g + 1) * 8, :], in0=S[:, g * 8:(g + 1) * 8, :], in1=stps[:], op=ALU.add),
                            reads=[b_S[g], b_st], writes=[b_S[g]])
                        P.op("act", lambda e: e.copy(out=prev[nxt][:, g * 8:(g + 1) * 8, :], in_=S[:, g * 8:(g + 1) * 8, :]),
                             reads=[b_S[g]], writes=[b_prev[nxt][g]])
                        if last:
                            tt = c // (TT // 128)
                            P.op("pool", lambda e: e.dma_start(
                                out=self.yT.rearrange("(q p) t -> p q t", p=128)[:, :, c * 128:(c + 1) * 128], in_=yo[i][:]),
                                reads=[b_yo[i]], writes=[bd["yT"][(tt, c % 4)]], dma=True)

                    units.append([u1, u2, u3])
            run_pipeline(units, 3, order=(1, 2, 0))

    def residual_out(self, tt, ms, wt, b_w, NK, rhs, b_rhs, ps_o, b_pso, hres, b_hres, hout, b_hout, h_in, h_out):
        P = self.P
        bd_in, bd_out = self.dbuf(h_in), self.dbuf(h_out)
        for m in ms:
            o = m % 2
            for kk in range(NK):
                P.op("pe", lambda e, kk=kk, m=m, o=o: e.matmul(
                    ps_o[o][:], lhsT=wt[:, kk, m * 128:(m + 1) * 128], rhs=rhs[:, kk, :],
                    start=(kk == 0), stop=(kk == NK - 1)),
                    reads=[b_rhs[kk], b_w[(kk, 0)]], writes=[b_pso[o]], inc=(kk == NK - 1))
            P.op("sp", lambda e, m=m, o=o: e.dma_start(
                out=hres[o][:], in_=getattr(self, h_in)[m * 128:(m + 1) * 128, tt * TT:(tt + 1) * TT]),
                reads=[bd_in[(tt, m)]], writes=[b_hres[o]], dma=True)
            P.op("dve", lambda e, o=o: e.tensor_tensor(
                out=hout[o][:], in0=hres[o][:], in1=ps_o[o][:], op=ALU.add),
                reads=[b_hres[o], b_pso[o]], writes=[b_hout[o]])
            P.op("pool", lambda e, m=m, o=o: e.dma_start(
                out=getattr(self, h_out)[m * 128:(m + 1) * 128, tt * TT:(tt + 1) * TT], in_=hout[o][:]),
                reads=[b_hout[o]], writes=[bd_out[(tt, m)]], dma=True)

    def phase_m_out(self, h_in, h_out):
        nc, P, NT, T = self.nc, self.P, self.NT, self.T
        bd = {n: self.dbuf(n) for n in ("zsT", "yT")}
        with ExitStack() as st:
            sb = lambda n, s, d: st.enter_context(nc.sbuf_tensor(f"mo_{n}", s, d))
            dbl = lambda n, s, d: [sb(f"{n}{i}", s, d) for i in range(2)]
            bufs = lambda n: [Buf(f"{n}{i}") for i in range(2)]
            wo = sb("wo", [128, 16, D], BF16); b_wo = Buf("wo")
            self.load_weight_bf16(wo, b_wo, self.w["ssm_out_w"], 16, D)
            yl = [sb(f"y{i}", [128, 4, TT], F32) for i in range(4)]; b_yl = [Buf(f"y{i}") for i in range(4)]
            zl = [sb(f"z{i}", [128, 4, TT], F32) for i in range(4)]; b_zl = [Buf(f"z{i}") for i in range(4)]
            sq2 = dbl("sq", [128, 4, TT], BF16); b_sq2 = bufs("sq")
            rstd2 = dbl("rstd", [128, TT], F32); b_rstd2 = bufs("rstd")
            yn2 = [sb(f"yn{i}", [128, 16, TT], BF16) for i in range(2)]; b_yn2 = [Buf(f"yn{i}") for i in range(2)]
            hres = dbl("hres", [128, TT], F32); b_hres = bufs("hres")
            hout = dbl("hout", [128, TT], F32); b_hout = bufs("hout")
            ps_stat2 = [st.enter_context(nc.psum_tensor(f"mo_stat{i}", [128, TT], F32)) for i in range(2)]
            b_stat2 = [Buf(f"stat{i}", True) for i in range(2)]
            ps_o = [st.enter_context(nc.psum_tensor(f"mo_o{i}", [128, TT], F32)) for i in range(2)]
            b_pso = [Buf(f"pso{i}", True) for i in range(2)]
            units = []
            for tt in range(NT + 1):
                for g in range(NG):
                    n_ = tt * NG + g
                    i4, i = n_ % 4, n_ % 2
                    if tt == NT:
                        units.append([None, None, None, None,
                                      (lambda tt=tt, g=g: self.residual_out(tt - 1, range(2 * g, 2 * g + 2), wo, b_wo, 16,
                                                                            yn2[(tt - 1) % 2], b_yn2[(tt - 1) % 2], ps_o, b_pso,
                                                                            hres, b_hres, hout, b_hout, h_in, h_out))])
                        continue
                    yn, b_yn = yn2[tt % 2], b_yn2[tt % 2]
                    tsl = slice(tt * TT, (tt + 1) * TT)

                    def g0(g=g, i4=i4, tt=tt, tsl=tsl):
                        for nm, dst, bb in (("yT", yl, b_yl), ("zsT", zl, b_zl)):
                            src = getattr(self, nm)[g * 512:(g + 1) * 512, tsl].rearrange("(k p) t -> p k t", p=128)
                            if nm == "yT":
                                rd = Prog.rows(bd[nm], tt, 4)
                            else:
                                rd = [bd[nm][(tt, g * 4 + k)] for k in range(4)]
                            P.op("sp", lambda e, src=src, d=dst[i4]: e.dma_start(out=d[:], in_=src),
                                 reads=rd, writes=[bb[i4]], dma=True)

                    def g1(i4=i4):
                        for k in range(4):
                            P.op("dve", lambda e, k=k: e.tensor_tensor(
                                out=yl[i4][:, k, :], in0=yl[i4][:, k, :], in1=zl[i4][:, k, :], op=ALU.mult),
                                reads=[b_yl[i4], b_zl[i4]], writes=[b_yl[i4][k]])

                    def g2(i=i, i4=i4):
                        self.rms_stats(yl[i4][:], b_yl[i4], sq2[i][:], [b_sq2[i]], ps_stat2[i], b_stat2[i], rstd2[i], b_rstd2[i], 4, 512.0)

                    def g3(g=g, i=i, i4=i4, yn=yn, b_yn=b_yn):
                        for k in range(4):
                            c = g * 4 + k
                            P.op("dve", lambda e, k=k, c=c: e.scalar_tensor_tensor(
                                out=yn[:, c, :], in0=yl[i4][:, k, :], scalar=self.vcol("gate_norm", c), in1=rstd2[i][:],
                                op0=ALU.mult, op1=ALU.mult), reads=[b_yl[i4], b_rstd2[i], self.b_const], writes=[b_yn[c]])

                    r4 = None
                    if tt >= 1:
                        r4 = (lambda tt=tt, g=g: self.residual_out(tt - 1, range(2 * g, 2 * g + 2), wo, b_wo, 16,
                                                                   yn2[(tt - 1) % 2], b_yn2[(tt - 1) % 2], ps_o, b_pso,
                                                                   hres, b_hres, hout, b_hout, h_in, h_out))
                    units.append([g0, g1, g2, g3, r4])
            run_pipeline(units, 5, oldest_first=True)

    def phase_a_qkv(self, h_in):
        nc, P, NT, T = self.nc, self.P, self.NT, self.T
        bd_in = self.dbuf(h_in)
        for n, shp in [("KT", [D, T]), ("QT", [D, T]), ("V_tm", [T, D])]:
            setattr(self, n, self.dram(n, shp, BF16))
        bd = {n: self.dbuf(n) for n in ("KT", "QT", "V_tm")}
        NB = TT // 128
        with ExitStack() as st:
            sb = lambda n, s, d: st.enter_context(nc.sbuf_tensor(f"aq_{n}", s, d))
            wk = sb("wk", [128, 8, D], BF16); b_wk = Buf("wk")
            wv = sb("wv", [128, 8, D], BF16); b_wv = Buf("wv")
            wq = sb("wq", [128, 8, D], BF16); b_wq = Buf("wq")
            self.load_weight_bf16(wk, b_wk, self.w["w_k"], 8, D)
            self.load_weight_bf16(wv, b_wv, self.w["w_v"], 8, D)
            self.load_weight_bf16(wq, b_wq, self.w["w_q"], 8, D)
            h_sb = [sb(f"h{i}", [128, 8, TT], F32) for i in range(2)]; b_h = [Buf(f"h{i}") for i in range(2)]
            sq = sb("sq", [128, 8, TT], BF16); b_sq = Buf("sq")
            rstd = sb("rstd", [128, TT], F32); b_rstd = Buf("rstd")
            uk2 = [sb(f"uk{i}", [128, 8, TT], BF16) for i in range(2)]; b_uk2 = [Buf(f"uk{i}") for i in range(2)]
            uq2 = [sb(f"uq{i}", [128, 8, TT], BF16) for i in range(2)]; b_uq2 = [Buf(f"uq{i}") for i in range(2)]
            ob = [sb(f"ob{i}", [128, TT], BF16) for i in range(2)]; b_ob = [Buf(f"ob{i}") for i in range(2)]
            vb = [sb(f"vb{i}", [128, D], BF16) for i in range(2)]; b_vb = [Buf(f"vb{i}") for i in range(2)]
            ps_stat = st.enter_context(nc.psum_tensor("aq_stat", [128, TT], F32)); b_stat = Buf("stat", True)
            ps_x = [st.enter_context(nc.psum_tensor(f"aq_x{i}", [128, TT], F32)) for i in range(4)]
            b_psx = [Buf(f"psx{i}", True) for i in range(4)]

            def load_h(tt):
                i = tt % 2
                P.op("sp", lambda e: e.dma_start(out=h_sb[i][:], in_=self.tile_ap(h_in, tt)),
                     reads=Prog.rows(bd_in, tt), writes=[b_h[i]], dma=True)

            def prologue(tt):
                i = tt % 2
                uk, uq, b_uk, b_uq = uk2[i], uq2[i], b_uk2[i], b_uq2[i]
                self.rms_stats(h_sb[i][:], b_h[i], sq[:], [b_sq], ps_stat, b_stat, rstd, b_rstd, 8, float(D))
                for k in range(8):
                    P.op("dve", lambda e, k=k: e.scalar_tensor_tensor(
                        out=uk[:, k, :], in0=h_sb[i][:, k, :], scalar=self.vcol("kv_norm", k), in1=rstd[:],
                        op0=ALU.mult, op1=ALU.mult), reads=[b_h[i], b_rstd, self.b_const], writes=[b_uk[k]])
                    P.op("dve", lambda e, k=k: e.scalar_tensor_tensor(
                        out=uq[:, k, :], in0=h_sb[i][:, k, :], scalar=self.vcol("attn_norm", k), in1=rstd[:],
                        op0=ALU.mult, op1=ALU.mult), reads=[b_h[i], b_rstd, self.b_const], writes=[b_uq[k]])
                if tt + 2 < NT:
                    load_h(tt + 2)

            load_h(0)
            if NT > 1:
                load_h(1)
            prologue(0)
            xi = 0
            oi = 0
            for tt in range(NT):
                i = tt % 2
                uk, uq, b_uk, b_uq = uk2[i], uq2[i], b_uk2[i], b_uq2[i]
                tsl = slice(tt * TT, (tt + 1) * TT)
                for which in range(2):
                    wt, b_w, uu, b_uu, dst, nm, scl = ((wk, b_wk, uk, b_uk, self.KT, "KT", 1.0),
                                                      (wq, b_wq, uq, b_uq, self.QT, "QT", 0.125))[which]
                    for m in range(8):
                        s_ = xi % 4
                        xi += 1
                        for k in range(8):
                            P.op("pe", lambda e, k=k, m=m, s_=s_, wt=wt, uu=uu: e.matmul(
                                ps_x[s_][:], lhsT=wt[:, k, m * 128:(m + 1) * 128], rhs=uu[:, k, :],
                                start=(k == 0), stop=(k == 7)),
                                reads=[b_uu[k], b_w[(k, 0)]], writes=[b_psx[s_]], inc=(k == 7))
                        o = oi % 2
                        oi += 1
                        P.op("act", lambda e, s_=s_, o=o, scl=scl: e.activation(out=ob[o][:], in_=ps_x[s_][:], func=AF.Copy, scale=scl),
                             reads=[b_psx[s_]], writes=[b_ob[o]])
                        P.op("pool", lambda e, m=m, o=o, dst=dst, tsl=tsl: e.dma_start(out=dst[m * 128:(m + 1) * 128, tsl], in_=ob[o][:]),
                             reads=[b_ob[o]], writes=[bd[nm][(tt, m)]], dma=True)
                    if which == 0 and tt + 1 < NT:
                        prologue(tt + 1)
                for blk in range(NB):
                    o = blk % 2
                    for half in range(2):
                        s_ = xi % 4
                        xi += 1
                        for k in range(8):
                            P.op("pe", lambda e, k=k, blk=blk, half=half, s_=s_, uk=uk: e.matmul(
                                ps_x[s_][:], lhsT=uk[:, k, blk * 128:(blk + 1) * 128], rhs=wv[:, k, half * 512:(half + 1) * 512],
                                start=(k == 0), stop=(k == 7)),
                                reads=[b_uk[k], b_wv[(k, 0)]], writes=[b_psx[s_]], inc=(k == 7))
                        P.op("act", lambda e, s_=s_, o=o, half=half: e.copy(out=vb[o][:, half * 512:(half + 1) * 512], in_=ps_x[s_][:]),
                             reads=[b_psx[s_]], writes=[b_vb[o][half]])
                    r0 = tt * TT + blk * 128
                    P.op("pool", lambda e, o=o, r0=r0: e.dma_start(out=self.V_tm[r0:r0 + 128, :], in_=vb[o][:]),
                         reads=[b_vb[o]], writes=[bd["V_tm"][(tt, blk)]], dma=True)

    def phase_attn(self):
        nc, P, T = self.nc, self.P, self.T
        NQG = T // TT
        NKB = T // 128
        self.oT = self.dram("oT", [D, T], BF16)
        bd = {n: self.dbuf(n) for n in ("KT", "QT", "V_tm", "oT")}
        with ExitStack() as st:
            sb = lambda n, s, d: st.enter_context(nc.sbuf_tensor(f"at_{n}", s, d))
            ring = lambda n, s, d, k: [sb(f"{n}{i}", s, d) for i in range(k)]
            bufs = lambda n, k: [Buf(f"{n}{i}") for i in range(k)]
            KTp = ring("KT", [128, T], BF16, 2); b_KT = bufs("KT", 2)
            QTp = ring("QT", [128, T], BF16, 2); b_QT = bufs("QT", 2)
            Vp = ring("V", [128, NKB, 128], BF16, 2); b_V = bufs("V", 2)
            NE, NS_, NA = 4, 5, 3
            ez = ring("ez", [128, 2, TT], F32, NE); b_ez = bufs("ez", NE)
            sp = ring("sp", [128, 2, TT], BF16, NS_); b_sp = bufs("sp", NS_)
            pm = ring("pm", [128, 2, TT], F32, 2); b_pm = bufs("pm", 2)
            att = ring("att", [128, 2, TT], BF16, NA); b_att = bufs("att", NA)
            osb = ring("o", [128, TT], BF16, 2); b_osb = bufs("o", 2)
            msk = sb("msk", [128, 4, TT], F32); b_msk = Buf("msk")
            P.op("sp", lambda e: e.dma_start(out=msk[:].rearrange("p a b -> p (a b)"), in_=self.consts_d[:, NCB * 128:NCONST * 128]),
                 writes=[b_msk], dma=True)
            pst = lambda n, s: st.enter_context(nc.psum_tensor(f"at_{n}", s, F32))
            zps = [pst(f"z{i}", [128, 2, TT]) for i in range(2)]; b_z = [Buf(f"z{i}", True) for i in range(2)]
            xps = pst("x", [128, 2, TT]); b_x = Buf("x", True)
            ops = [pst(f"op{i}", [128, TT]) for i in range(2)]; b_o = [Buf(f"op{i}", True) for i in range(2)]
            TGE, TLT = self.cb[:, 4, :], self.cb[:, 5, :]

            def load_pair(p):
                i = p % 2
                P.op("sp", lambda e: e.dma_start(out=KTp[i][:], in_=self.KT[p * 128:(p + 1) * 128, :]),
                     reads=[bd["KT"]], writes=[b_KT[i]], dma=True)
                P.op("sp", lambda e: e.dma_start(out=QTp[i][:], in_=self.QT[p * 128:(p + 1) * 128, :]),
                     reads=[bd["QT"]], writes=[b_QT[i]], dma=True)
                P.op("sp", lambda e: e.dma_start(
                    out=Vp[i][:], in_=self.V_tm[:, p * 128:(p + 1) * 128].rearrange("(b p) d -> p b d", p=128)),
                    reads=[bd["V_tm"]], writes=[b_V[i]], dma=True)

            steps = []
            for p in range(AH // 2):
                for qg in range(NQG):
                    kbs = list(range(qg * 4 + 3, -1, -1))
                    for n, kb in enumerate(kbs):
                        steps.append((p, qg, kb, n == 0, n == len(kbs) - 1))

            def stage1(si):
                p, qg, kb, first, last = steps[si]
                pi, r, z = p % 2, si % NE, si % 2
                for hd in range(2):
                    rows = slice(hd * 64, (hd + 1) * 64)
                    P.op("pe", lambda e, hd=hd, rows=rows: e.matmul(
                        zps[z][:, hd, :], lhsT=KTp[pi][rows, kb * 128:(kb + 1) * 128],
                        rhs=QTp[pi][rows, qg * TT:(qg + 1) * TT], start=True, stop=True),
                        reads=[b_KT[pi], b_QT[pi]], writes=[b_z[z]], inc=(hd == 1))
                j = kb - qg * 4
                c0 = max(j, 0) * 128
                P.op("act", lambda e: e.activation(out=ez[r][:, :, c0:], in_=zps[z][:, :, c0:], func=AF.Exp),
                     reads=[b_z[z]], writes=[b_ez[r]])
                if j >= 0:
                    P.op("dve", lambda e: e.tensor_tensor(out=ez[r][:, :, c0:], in0=ez[r][:, :, c0:],
                                                          in1=msk[:, j, c0:].unsqueeze(1).to_broadcast([128, 2, TT - c0]), op=ALU.mult),
                         reads=[b_ez[r], b_msk], writes=[b_ez[r]])

            def stage1b(si):
                p, qg, kb, first, last = steps[si]
                r, rs = si % NE, si % NS_
                c0 = max(kb - qg * 4, 0) * 128
                if c0 > 0:
                    P.op("pool", lambda e: e.memset(sp[rs][:, :, 0:c0], 0.0), writes=[b_sp[rs]])
                P.op("act", lambda e: e.activation(out=sp[rs][:, :, c0:], in_=ez[r][:, :, c0:], func=AF.Ln, bias=self.vcol("one")),
                     reads=[b_ez[r], self.b_const], writes=[b_sp[rs]["v"]])

            def stage2(si):
                p, qg, kb, first, last = steps[si]
                r, rs, rp, ra = si % NE, si % NS_, (si - 1) % NS_, si % NA
                for hd in range(2):
                    P.op("pe", lambda e, hd=hd: e.matmul(xps[:, hd, :], lhsT=TGE, rhs=sp[rs][:, hd, :],
                                                         start=first, stop=first, skip_group_check=True),
                         reads=[b_sp[rs], self.b_const], writes=[b_x], inc=(first and hd == 1))
                if not first:
                    for hd in range(2):
                        P.op("pe", lambda e, hd=hd: e.matmul(xps[:, hd, :], lhsT=TLT, rhs=sp[rp][:, hd, :],
                                                             start=False, stop=True, skip_group_check=True),
                             reads=[b_sp[rp], self.b_const], writes=[b_x], inc=(hd == 1))
                pi_ = si % 2
                c0 = max(kb - qg * 4, 0) * 128
                P.op("act", lambda e: e.activation(out=pm[pi_][:, :, c0:], in_=xps[:, :, c0:], func=AF.Exp, scale=-1.0),
                     reads=[b_x], writes=[b_pm[pi_]])
                if c0 > 0:
                    P.op("pool", lambda e: e.memset(att[ra][:, :, 0:c0], 0.0), writes=[b_att[ra]])
                P.op("dve", lambda e: e.tensor_tensor(out=att[ra][:, :, c0:], in0=ez[r][:, :, c0:], in1=pm[pi_][:, :, c0:], op=ALU.mult),
                     reads=[b_ez[r], b_pm[pi_]], writes=[b_att[ra]["v"]])

            def stage3(si):
                p, qg, kb, first, last = steps[si]
                pi, ra = p % 2, si % NA
                o = (p * NQG + qg) % 2
                for hd in range(2):
                    rows = slice(hd * 64, (hd + 1) * 64)
                    P.op("pe", lambda e, hd=hd, rows=rows: e.matmul(
                        ops[o][rows, :], lhsT=Vp[pi][:, kb, rows], rhs=att[ra][:, hd, :], start=first, stop=last,
                        tile_position=((0, 64) if hd else None)),
                        reads=[b_V[pi], b_att[ra]], writes=[b_o[o]], inc=(hd == 1))
                if last and qg == NQG - 1 and p + 2 < AH // 2:
                    load_pair(p + 2)
                if last:
                    P.op("act", lambda e: e.copy(out=osb[o][:], in_=ops[o][:]), reads=[b_o[o]], writes=[b_osb[o]])
                    P.op("pool", lambda e: e.dma_start(out=self.oT[p * 128:(p + 1) * 128, qg * TT:(qg + 1) * TT], in_=osb[o][:]),
                         reads=[b_osb[o]], writes=[bd["oT"][(qg, p)]], dma=True)

            n = len(steps)
            L2, L3 = 2, 3
            load_pair(0)
            load_pair(1)
            for i in range(n + L3):
                if i < n:
                    stage1(i)
                if 0 <= i - L2 < n:
                    stage2(i - L2)
                if 0 <= i - L3 < n:
                    stage3(i - L3)
                if i < n:
                    stage1b(i)

    def phase_a_out(self, h_in, h_out):
        nc, P, NT = self.nc, self.P, self.NT
        bd_o = self.dbuf("oT")
        with ExitStack() as st:
            sb = lambda n, s, d: st.enter_context(nc.sbuf_tensor(f"ao_{n}", s, d))
            wo = sb("wo", [128, 8, D], BF16); b_wo = Buf("wo")
            self.load_weight_bf16(wo, b_wo, self.w["w_o"], 8, D)
            ot = [sb(f"ot{i}", [128, 8, TT], BF16) for i in range(2)]; b_ot = [Buf(f"ot{i}") for i in range(2)]
            hres = [sb(f"hres{i}", [128, TT], F32) for i in range(2)]; b_hres = [Buf(f"hres{i}") for i in range(2)]
            hout = [sb(f"hout{i}", [128, TT], F32) for i in range(2)]; b_hout = [Buf(f"hout{i}") for i in range(2)]
            ps_o = [st.enter_context(nc.psum_tensor(f"ao_o{i}", [128, TT], F32)) for i in range(2)]
            b_pso = [Buf(f"pso{i}", True) for i in range(2)]
            for tt in range(NT):
                i = tt % 2
                P.op("sp", lambda e, i=i, tt=tt: e.dma_start(out=ot[i][:], in_=self.tile_ap("oT", tt)),
                     reads=Prog.rows(bd_o, tt), writes=[b_ot[i]], dma=True)
                self.residual_out(tt, range(8), wo, b_wo, 8, ot[i], b_ot[i], ps_o, b_pso, hres, b_hres, hout, b_hout, h_in, h_out)

    def phase_norm(self, h_in, h_out, wname, is_output):
        nc, P, NT = self.nc, self.P, self.NT
        bd_in, bd_out = self.dbuf(h_in), self.dbuf(h_out)
        with ExitStack() as st:
            sb = lambda n, s, d: st.enter_context(nc.sbuf_tensor(f"n_{n}", s, d))
            h_sb = [sb(f"h{i}", [128, 8, TT], F32) for i in range(2)]; b_h = [Buf(f"h{i}") for i in range(2)]
            sq = sb("sq", [128, 8, TT], BF16); b_sq = Buf("sq")
            rstd = sb("rstd", [128, TT], F32); b_rstd = Buf("rstd")
            ps_stat = st.enter_context(nc.psum_tensor("n_stat", [128, TT], F32)); b_stat = Buf("stat", True)
            for tt in range(NT):
                i = tt % 2
                P.op("sp", lambda e, i=i, tt=tt: e.dma_start(out=h_sb[i][:], in_=self.tile_ap(h_in, tt)),
                     reads=Prog.rows(bd_in, tt), writes=[b_h[i]], dma=True)
                self.rms_stats(h_sb[i][:], b_h[i], sq[:], [b_sq], ps_stat, b_stat, rstd, b_rstd, 8, float(D))
                for k in range(8):
                    P.op("dve", lambda e, k=k, i=i: e.scalar_tensor_tensor(
                        out=h_sb[i][:, k, :], in0=h_sb[i][:, k, :], scalar=self.vcol(wname, k), in1=rstd[:],
                        op0=ALU.mult, op1=ALU.mult), reads=[b_h[i], b_rstd, self.b_const], writes=[b_h[i][k]])
                tok = P.op("pool", lambda e, i=i, tt=tt: e.dma_start(out=self.tile_ap(h_out, tt), in_=h_sb[i][:]),
                           reads=[b_h[i]], writes=Prog.rows(bd_out, tt), dma=True)
                if is_output:
                    self.out_toks.append(tok)


WNAMES = ["ssm_in_w", "ssm_out_w", "w_k", "w_v", "w_q", "w_o", "ffn_up_w0", "ffn_up_w1", "ffn_down_w0",
          "ffn_down_w1"]


def get_weight(inp, name):
    if name in ("w_k", "w_v"):
        a = inp[name]
    elif name[-1] in "01" and name.startswith("ffn"):
        a = inp[name[:-1]][int(name[-1])]
    else:
        a = inp[name][0]
    return np.ascontiguousarray(np.asarray(a, np.float32))


FULL_PHASES = [
    ("m_in", "xT"), ("ssd",), ("m_out", "xT", "hA"), ("ffn", 0, "hA", "hB"),
    ("a_qkv", "hB"), ("attn",), ("a_out", "hB", "hA"), ("ffn", 1, "hA", "hB"),
    ("norm", "hB", "outT", "final_norm", True),
]

_CACHE = {}


def kernel(**inputs):
    x = np.asarray(inputs["x"], np.float32)
    B, T, _ = x.shape
    assert B == NCORES and T % TT == 0
    if T not in _CACHE:
        _CACHE[T] = Builder(T, FULL_PHASES).build()
    nc = _CACHE[T]
    shared = {"vecs": pack_vecs(inputs), "consts": make_consts()}
    for name in WNAMES:
        shared[name] = get_weight(inputs, name)
    in_maps = []
    for b in range(B):
        m = dict(shared)
        m["xT"] = np.ascontiguousarray(x[b].T)
        in_maps.append(m)
    res = run_bass_kernel_spmd(nc, in_maps, core_ids=list(range(NCORES)))
    out = np.stack([np.ascontiguousarray(np.asarray(r["outT"], np.float32).T) for r in res.results], axis=0)
    return out
```
